# Optimizing a Trainium2 kernel written in Bass

```python
import jax, jax.numpy as jnp
from jax import lax
import numpy as np

D_MODEL = 2048
BATCH = 1
SEQ = 16384
DEPTH = 1

CTX_LEN = 256
GRID_W = 64
CHUNK = 128
RET_HEAD_DIM = 128
RET_HEADS = D_MODEL // RET_HEAD_DIM
RET_QK = RET_HEADS * RET_HEAD_DIM
RET_V = RET_HEADS * RET_HEAD_DIM
SSD_INNER = 2 * D_MODEL
SSD_HEAD_DIM = 64
SSD_HEADS = SSD_INNER // SSD_HEAD_DIM
SSD_STATE = 128
SSD_GROUPS = 8
SSD_CONV = 5
SSD_CONV_CH = SSD_INNER + 2 * SSD_GROUPS * SSD_STATE
D_FF = 4 * D_MODEL
ROPE_BASE = 10000.0
EPS = 1e-6
IN_SIZES = (RET_QK, RET_QK, RET_V, RET_V, SSD_INNER, SSD_CONV_CH, 2 * SSD_HEADS, 2 * D_MODEL)
IN_COLS = sum(IN_SIZES)
IN_SPLITS = tuple(int(s) for s in np.cumsum(IN_SIZES)[:-1])

kernel_name = "hybrid_retention_ssd_prefix_dit"


def rms_norm(t, w):
    tf = t.astype(jnp.float32)
    y = tf * lax.rsqrt(jnp.mean(tf * tf, axis=-1, keepdims=True) + EPS)
    return (y * w.astype(jnp.float32)).astype(t.dtype)


def head_layer_norm(y):
    yf = y.astype(jnp.float32)
    mu = jnp.mean(yf, axis=-1, keepdims=True)
    d = yf - mu
    return d * lax.rsqrt(jnp.mean(d * d, axis=-1, keepdims=True) + EPS)


def rope_angles(L):
    rows = L // GRID_W
    row = jnp.repeat(jnp.arange(rows), GRID_W).astype(jnp.float32)
    col = jnp.tile(jnp.arange(GRID_W), rows).astype(jnp.float32)
    n_freq = RET_HEAD_DIM // 4
    freqs = ROPE_BASE ** (-jnp.arange(n_freq, dtype=jnp.float32) / n_freq)
    return row[:, None] * freqs, col[:, None] * freqs


def rotate(t, ang):
    cos = jnp.cos(ang)[None, :, None, :]
    sin = jnp.sin(ang)[None, :, None, :]
    t1, t2 = jnp.split(t, 2, axis=-1)
    return jnp.concatenate([t1 * cos - t2 * sin, t1 * sin + t2 * cos], axis=-1).astype(t.dtype)


def axial_rope(t, ang_row, ang_col):
    half = RET_HEAD_DIM // 2
    return jnp.concatenate([rotate(t[..., :half], ang_row), rotate(t[..., half:], ang_col)], axis=-1)


def chunked_scan(q, k, v, log_a, h0, strict, need_output):
    bsz, L, G, N = q.shape
    H, P = v.shape[2], v.shape[3]
    r = H // G
    nc = L // CHUNK
    q = q.reshape(bsz, nc, CHUNK, G, N)
    k = k.reshape(bsz, nc, CHUNK, G, N)
    v = v.reshape(bsz, nc, CHUNK, G, r, P)
    acs = jnp.cumsum(log_a.astype(jnp.float32).reshape(bsz, nc, CHUNK, G, r), axis=2)
    to_end = jnp.exp(acs[:, :, -1:] - acs)
    chunk_state = jnp.einsum('bcjgn,bcjgrp->bcgrpn', k, v * to_end[..., None]).astype(jnp.float32)
    chunk_decay = jnp.exp(acs[:, :, -1])

    def step(h, inp):
        st, dec = inp
        return dec[..., None, None] * h + st, h

    h_last, h_in = lax.scan(step, h0.astype(jnp.float32).reshape(bsz, G, r, P, N),
                            (jnp.moveaxis(chunk_state, 1, 0), jnp.moveaxis(chunk_decay, 1, 0)))
    h_last = h_last.reshape(bsz, H, P, N)
    if not need_output:
        return None, h_last
    h_in = jnp.moveaxis(h_in, 0, 1)
    y_cross = jnp.einsum('bcign,bcgrpn->bcigrp', q, h_in) * jnp.exp(acs)[..., None]
    acs_t = jnp.moveaxis(acs, 2, -1)
    mask = jnp.tril(jnp.ones((CHUNK, CHUNK), dtype=bool), -1 if strict else 0)
    decay = jnp.exp(jnp.where(mask, acs_t[..., :, None] - acs_t[..., None, :], -jnp.inf))
    scores = jnp.einsum('bcign,bcjgn->bcgij', q, k)
    y_intra = jnp.einsum('bcgrij,bcjgrp->bcigrp', scores[:, :, :, None] * decay, v)
    return (y_intra + y_cross).reshape(bsz, L, H, P), h_last


def bidir_scan(q, k, v_f, v_b, la_f, la_b, h0_f, h0_b, need_output):
    y_f, h_f = chunked_scan(q, k, v_f, la_f, h0_f, False, need_output)
    fl = lambda t: jnp.flip(t, axis=1)
    y_b, h_b = chunked_scan(fl(q), fl(k), fl(v_b), fl(la_b), h0_b, True, need_output)
    y = y_f + fl(y_b) if need_output else None
    return y, h_f, h_b


def centred_dwconv(t, w, b):
    pad = SSD_CONV // 2
    L = t.shape[1]
    tp = jnp.pad(t, ((0, 0), (pad, pad), (0, 0)))
    out = b
    for j in range(SSD_CONV):
        out = out + tp[:, j:j + L] * w[j]
    return out


def token_mixers(u, w_in, conv_w, conv_b, ret_decay_logit, ret_norm_w, ssd_a_log, ssd_dt_bias, ssd_d,
                 ssd_norm_w, w_ret_out, w_ssd_out, rope, init, need_output):
    bsz, L, _ = u.shape
    q, k, v, g_ret, z, xbc, dt_raw, branch_logits = jnp.split(u @ w_in, IN_SPLITS, axis=-1)
    q = q.reshape(bsz, L, RET_HEADS, RET_HEAD_DIM)
    k = k.reshape(bsz, L, RET_HEADS, RET_HEAD_DIM) * (RET_HEAD_DIM ** -0.5)
    if rope is not None:
        q = axial_rope(q, *rope)
        k = axial_rope(k, *rope)
    v = v.reshape(bsz, L, RET_HEADS, RET_HEAD_DIM)
    log_gamma = jax.nn.log_sigmoid(ret_decay_logit.astype(jnp.float32))
    la_rf = jnp.broadcast_to(log_gamma[0], (bsz, L, RET_HEADS))
    la_rb = jnp.broadcast_to(log_gamma[1], (bsz, L, RET_HEADS))
    y_ret, r_f, r_b = bidir_scan(q, k, v, v, la_rf, la_rb, init[0], init[1], need_output)
    xbc = jax.nn.silu(centred_dwconv(xbc, conv_w, conv_b))
    xs, b_in, c_out = jnp.split(xbc, [SSD_INNER, SSD_INNER + SSD_GROUPS * SSD_STATE], axis=-1)
    xs = xs.reshape(bsz, L, SSD_HEADS, SSD_HEAD_DIM)
    b_in = b_in.reshape(bsz, L, SSD_GROUPS, SSD_STATE)
    c_out = c_out.reshape(bsz, L, SSD_GROUPS, SSD_STATE)
    dt = jax.nn.softplus(dt_raw.astype(jnp.float32).reshape(bsz, L, 2, SSD_HEADS) + ssd_dt_bias.astype(jnp.float32))
    a = -jnp.exp(ssd_a_log.astype(jnp.float32))
    y_ssd, s_f, s_b = bidir_scan(c_out, b_in, xs * dt[:, :, 0, :, None], xs * dt[:, :, 1, :, None],
                                 dt[:, :, 0] * a[0], dt[:, :, 1] * a[1], init[2], init[3], need_output)
    states = (r_f, r_b, s_f, s_b)
    if not need_output:
        return None, states
    yr = head_layer_norm(y_ret).reshape(bsz, L, RET_V) * ret_norm_w
    yr = (yr * jax.nn.silu(g_ret.astype(jnp.float32))).astype(u.dtype) @ w_ret_out
    ys = (y_ssd + xs * ssd_d[:, None]).reshape(bsz, L, SSD_INNER)
    ys = rms_norm((ys * jax.nn.silu(z.astype(jnp.float32))).astype(u.dtype), ssd_norm_w) @ w_ssd_out
    g_r, g_s = jnp.split(branch_logits, 2, axis=-1)
    m = jax.nn.sigmoid(g_r) * yr + jax.nn.sigmoid(g_s) * ys
    return m.astype(u.dtype), states


def sq_relu_mlp(t, w1, w2):
    return jnp.square(jax.nn.relu(t @ w1)) @ w2


def setup_inputs(seed: int = 0) -> dict:
    key = jax.random.key(seed)
    ks = jax.random.split(key, 32)
    f32 = jnp.float32
    nrm = lambda kk, shape, fan_in: jax.random.normal(kk, shape, f32) * (fan_in ** -0.5)
    gain = lambda kk, shape: 1.0 + 0.02 * jax.random.normal(kk, shape, f32)
    p = 2.0 ** (-5.0 - jnp.arange(RET_HEADS, dtype=f32))
    logit0 = jnp.log1p(-p) - jnp.log(p)
    ret_decay_logit = jnp.stack([logit0, logit0[::-1]])[None] + 0.01 * jax.random.normal(ks[9], (DEPTH, 2, RET_HEADS), f32)
    dt0 = jnp.exp(jax.random.uniform(ks[12], (DEPTH, 2, SSD_HEADS), f32, jnp.log(1e-3), jnp.log(1e-1)))
    return {
        "x": jax.random.normal(ks[0], (BATCH, SEQ, D_MODEL), f32),
        "c": jax.random.normal(ks[1], (BATCH, D_MODEL), f32),
        "ctx": jax.random.normal(ks[2], (BATCH, CTX_LEN, D_MODEL), f32),
        "c_ctx": jax.random.normal(ks[3], (D_MODEL,), f32),
        "w_mod": 0.5 * nrm(ks[4], (DEPTH, D_MODEL, 6 * D_MODEL), D_MODEL),
        "b_mod": 0.02 * jax.random.normal(ks[5], (DEPTH, 6 * D_MODEL), f32),
        "norm1_w": gain(ks[6], (DEPTH, D_MODEL)),
        "w_in": nrm(ks[7], (DEPTH, D_MODEL, IN_COLS), D_MODEL),
        "conv_w": nrm(ks[8], (DEPTH, SSD_CONV, SSD_CONV_CH), SSD_CONV),
        "conv_b": 0.02 * jax.random.normal(ks[10], (DEPTH, SSD_CONV_CH), f32),
        "ret_decay_logit": ret_decay_logit,
        "ret_norm_w": gain(ks[11], (DEPTH, RET_V)),
        "ssd_a_log": jnp.log(jax.random.uniform(ks[13], (DEPTH, 2, SSD_HEADS), f32, 1.0, 16.0)),
        "ssd_dt_bias": dt0 + jnp.log(-jnp.expm1(-dt0)),
        "ssd_d": 1.0 + 0.1 * jax.random.normal(ks[14], (DEPTH, SSD_HEADS), f32),
        "ssd_norm_w": gain(ks[15], (DEPTH, SSD_INNER)),
        "w_ret_out": nrm(ks[16], (DEPTH, RET_V, D_MODEL), RET_V),
        "w_ssd_out": nrm(ks[17], (DEPTH, SSD_INNER, D_MODEL), SSD_INNER),
        "w_o": nrm(ks[18], (DEPTH, D_MODEL, D_MODEL), D_MODEL),
        "norm2_w": gain(ks[19], (DEPTH, D_MODEL)),
        "w_mlp1": nrm(ks[20], (DEPTH, D_MODEL, D_FF), D_MODEL),
        "w_mlp2": nrm(ks[21], (DEPTH, D_FF, D_MODEL), D_FF),
        "final_norm_w": gain(ks[22], (D_MODEL,)),
    }


def reference(x, c, ctx, c_ctx, w_mod, b_mod, norm1_w, w_in, conv_w, conv_b, ret_decay_logit, ret_norm_w,
              ssd_a_log, ssd_dt_bias, ssd_d, ssd_norm_w, w_ret_out, w_ssd_out, w_o, norm2_w, w_mlp1, w_mlp2,
              final_norm_w):
    bsz, L, _ = x.shape
    rope = rope_angles(L)
    zeros_r = jnp.zeros((bsz, RET_HEADS, RET_HEAD_DIM, RET_HEAD_DIM), jnp.float32)
    zeros_s = jnp.zeros((bsz, SSD_HEADS, SSD_HEAD_DIM, SSD_STATE), jnp.float32)
    ctx_init = (zeros_r, zeros_r, zeros_s, zeros_s)
    h, hc = x, ctx
    for layer in range(DEPTH):
        last = layer == DEPTH - 1
        mod = jax.nn.silu(c) @ w_mod[layer] + b_mod[layer]
        mod_c = jax.nn.silu(c_ctx) @ w_mod[layer] + b_mod[layer]
        sh_a, sc_a, g_a, sh_f, sc_f, g_f = jnp.split(mod[:, None, :], 6, axis=-1)
        csh_a, csc_a, cg_a, csh_f, csc_f, cg_f = jnp.split(mod_c, 6, axis=-1)
        mix_w = (w_in[layer], conv_w[layer], conv_b[layer], ret_decay_logit[layer], ret_norm_w[layer],
                 ssd_a_log[layer], ssd_dt_bias[layer], ssd_d[layer], ssd_norm_w[layer],
                 w_ret_out[layer], w_ssd_out[layer])
        uc = rms_norm(hc, norm1_w[layer]) * (1.0 + csc_a) + csh_a
        mc, ctx_states = token_mixers(uc, *mix_w, None, ctx_init, not last)
        u = rms_norm(h, norm1_w[layer]) * (1.0 + sc_a) + sh_a
        m, _ = token_mixers(u, *mix_w, rope, ctx_states, True)
        h = h + g_a * (m @ w_o[layer])
        f = rms_norm(h, norm2_w[layer]) * (1.0 + sc_f) + sh_f
        h = h + g_f * sq_relu_mlp(f, w_mlp1[layer], w_mlp2[layer])
        if not last:
            hc = hc + cg_a * (mc @ w_o[layer])
            fc = rms_norm(hc, norm2_w[layer]) * (1.0 + csc_f) + csh_f
            hc = hc + cg_f * sq_relu_mlp(fc, w_mlp1[layer], w_mlp2[layer])
    return rms_norm(h, final_norm_w)
```

```python
import numpy as np
import concourse.bass as bass
import concourse.mybir as mybir
from concourse.bass_utils import run_bass_kernel_spmd

F32 = mybir.dt.float32
BF16 = mybir.dt.bfloat16
AF = mybir.ActivationFunctionType
ALU = mybir.AluOpType
AX = mybir.AxisListType

NCORES = 8
D = 2048
T = 2048
NCH = T // 128
CTX = 256
KC = D // 128
EPS = 1e-6
IN_COLS = 22656
C_Q, C_K, C_V, C_G, C_Z, C_XBC, C_DT, C_BL = 0, 2048, 4096, 6144, 8192, 12288, 18432, 18560
DFF = 8192
DEBUG_OUT = ()


class Prog:
    ENGS = ("pe", "act", "dve", "pool", "sp")

    def __init__(self, nc):
        self.nc = nc
        self.ops = []

    def op(self, eng, fn, reads=(), writes=(), dma=None):
        self.ops.append((eng, fn, tuple(reads), tuple(writes), dma, False))

    def barrier(self):
        for e in self.ENGS:
            self.ops.append((e, None, (), (), None, True))

    def emit(self):
        nc = self.nc
        ops = self.ops
        n = len(ops)
        last_w, readers = {}, {}
        deps = [None] * n
        has_dep = [False] * n
        last_of_eng = {}
        dma_since = []
        for i, (eng, fn, reads, writes, dma, bar) in enumerate(ops):
            if bar:
                dd = [j for e2, j in last_of_eng.items() if e2 != eng]
                dd += dma_since
                deps[i] = dd
                for j in dd:
                    has_dep[j] = True
                continue
            d = set()
            for r in reads:
                j = last_w.get(r)
                if j is not None:
                    d.add(j)
            for w in writes:
                j = last_w.get(w)
                if j is not None:
                    d.add(j)
                for rr in readers.get(w, ()):
                    d.add(rr)
            d.discard(i)
            dd = []
            for j in d:
                ej, _, rj, wj, dj, bj = ops[j]
                if dj is None and dma is None and ej == eng:
                    if eng == "pe":
                        continue
                    if not any((w in reads) for w in wj):
                        continue
                dd.append(j)
            deps[i] = dd
            for j in dd:
                has_dep[j] = True
            for r in reads:
                readers.setdefault(r, []).append(i)
            for w in writes:
                last_w[w] = i
                readers[w] = []
            if dma is not None:
                dma_since.append(i)
            else:
                last_of_eng[eng] = i
        eng_sem = {e: nc.alloc_semaphore("s_" + e) for e in self.ENGS}
        dma_sem, dma_cnt = {}, {}
        sem_pool = {"sp": [nc.alloc_semaphore("dsp%d" % k) for k in range(36)],
                    "pool": [nc.alloc_semaphore("dpl%d" % k) for k in range(6)]}
        sem_pool["act"] = sem_pool["sp"]
        nkeys = {"sp": 0, "pool": 0, "act": 0}
        eng_cnt = {e: 0 for e in self.ENGS}
        token = [None] * n
        for i, (eng, fn, reads, writes, dma, bar) in enumerate(ops):
            if dma is not None:
                if dma not in dma_sem:
                    pool_ = sem_pool[eng]
                    dma_sem[dma] = pool_[nkeys[eng] % len(pool_)]
                    nkeys[eng] += 1
                sm = dma_sem[dma]
                dma_cnt[id(sm)] = dma_cnt.get(id(sm), 0) + 16
                token[i] = (sm, dma_cnt[id(sm)])
            elif has_dep[i]:
                eng_cnt[eng] += 1
                token[i] = (eng_sem[eng], eng_cnt[eng])
        self.n_sems = len(dma_sem) + 5
        per_eng = {e: [] for e in self.ENGS}
        for i, o in enumerate(ops):
            per_eng[o[0]].append(i)

        def run(e, engobj):
            waited = {}
            for i in per_eng[e]:
                eng, fn, reads, writes, dma, bar = ops[i]
                for j in deps[i]:
                    sem, val = token[j]
                    k = id(sem)
                    if waited.get(k, 0) >= val:
                        continue
                    waited[k] = val
                    engobj.wait_ge(sem, val)
                if bar:
                    continue
                if dma is not None:
                    sem, val = token[i]
                    if val > 16 and waited.get(id(sem), 0) < val - 16:
                        waited[id(sem)] = val - 16
                        engobj.wait_ge(sem, val - 16)
                ins = fn(engobj)
                if token[i] is not None:
                    sem, val = token[i]
                    ins.then_inc(sem, 16 if dma is not None else 1)

        with nc.Block() as block:
            @block.tensor
            def _(e):
                run("pe", e)

            @block.scalar
            def _(e):
                run("act", e)

            @block.vector
            def _(e):
                run("dve", e)

            @block.gpsimd
            def _(e):
                run("pool", e)

            @block.sync
            def _(e):
                run("sp", e)


class Arena:
    def __init__(self, nc, name, nbytes):
        self.t = nc.alloc_sbuf_tensor(name, [128, nbytes // 2], BF16)
        self.cap = nbytes // 2
        self.off = 0
        self.n = 0
        self.name = name

    def reset(self):
        self.off = 0

    def get(self, free_shape, dt, tag=None):
        ne = int(np.prod(free_shape))
        el = ne * (2 if dt == F32 else 1)
        self.off = (self.off + 15) // 16 * 16
        assert self.off + el <= self.cap, (self.name, tag, self.off, el, self.cap)
        v = self.t[:, self.off:self.off + el]
        self.off += el
        if dt == F32:
            v = v.bitcast(F32)
        if len(free_shape) == 2:
            v = v.rearrange("p (a b) -> p a b", a=free_shape[0])
        elif len(free_shape) == 3:
            v = v.rearrange("p (a b c) -> p a b c", a=free_shape[0], b=free_shape[1])
        self.n += 1
        return v, "%s_%s_%d" % (self.name, tag or "t", self.n)


class Ring:
    def __init__(self, arena, n, free_shape, dt, tag):
        self.items = [arena.get(free_shape, dt, tag) for _ in range(n)]
        self.i = 0

    def next(self):
        it = self.items[self.i % len(self.items)]
        self.i += 1
        return it


def bc(ap_small, shape):
    return ap_small.to_broadcast(list(shape))


def build_program():
    nc = bass.Bass("TRN2", target_bir_lowering=False)
    P = Prog(nc)

    def din(name, shape, dt=F32):
        return nc.dram_tensor(name, list(shape), dt, kind="ExternalInput").ap()

    def dscr(name, shape, dt):
        if name in DEBUG_OUT:
            return nc.dram_tensor(name, list(shape), dt, kind="ExternalOutput").ap()
        return nc.dram_tensor(name, list(shape), dt).ap()

    x_d = din("x", [T, D]); xh_d = din("x_halo", [4, D]); ctx_d = din("ctx", [CTX, D]); cvec_d = din("cvec", [2, D])
    w_mod_d = din("w_mod", [D, 6 * D]); b_mod_d = din("b_mod", [128, 96]); n1w_d = din("norm1_w", [128, 16])
    w_in_d = din("w_in", [D, IN_COLS]); conv_w_d = din("conv_w", [128, 48, 5]); conv_b_d = din("conv_b", [128, 48])
    rdl_d = din("ret_decay_logit", [32]); rnw_d = din("ret_norm_w", [D]); alog_d = din("ssd_a_log", [128])
    dtb_d = din("ssd_dt_bias", [128]); ssdd_d = din("ssd_d", [64]); snw_d = din("ssd_norm_w", [4096])
    w_ro_d = din("w_ret_out", [D, D]); w_so_d = din("w_ssd_out", [4096, D]); w_o_d = din("w_o", [D, D])
    n2w_d = din("norm2_w", [128, 16]); w1_d = din("w_mlp1", [D, DFF]); w2_d = din("w_mlp2", [DFF, D]); fnw_d = din("final_norm_w", [D])
    cst_d = din("cst", [128, 8 * 128]); rope_d = din("rope", [128, 2, NCH, 64]); core_d = din("corec", [128, 2 * 9 * 8 + 2 * 9 + 4])
    out_d = nc.dram_tensor("out", [T, D], F32, kind="ExternalOutput").ap()

    qT_s = dscr("qT_s", [NCH, 128, 16, 128], BF16); kT_s = dscr("kT_s", [NCH, 128, 16, 128], BF16)
    ktok_s = dscr("ktok_s", [T, D], BF16); v_s = dscr("v_s", [T, D], BF16); g_s = dscr("g_s", [T, D], BF16)
    z_s = dscr("z_s", [T, 4096], BF16); bl_s = dscr("bl_s", [T, 4096], BF16); dt_s = dscr("dt_s", [T, 128], F32)
    xs_s = dscr("xs_s", [T, 4096], BF16); btok_s = dscr("btok_s", [T, 1024], BF16)
    bT_s = dscr("bT_s", [NCH, 128, 8, 128], BF16); cT_s = dscr("cT_s", [NCH, 128, 8, 128], BF16)
    cktok_s = dscr("cktok_s", [CTX, D], BF16); cv_s = dscr("cv_s", [CTX, D], BF16); cdt_s = dscr("cdt_s", [CTX, 128], F32)
    cxs_s = dscr("cxs_s", [CTX, 4096], BF16); cbtok_s = dscr("cbtok_s", [CTX, 1024], BF16)
    y_s = dscr("y_s", [T, 6144], F32)
    yrT_s = dscr("yrT_s", [16, 128, T], BF16); ysT_s = dscr("ysT_s", [32, 128, T], BF16)
    mT_s = dscr("mT_s", [16, 128, T], BF16); h1_s = dscr("h1_s", [T, D], F32); fT_s = dscr("fT_s", [16, 128, T], BF16)
    hidT_s = dscr("hidT_s", [64, 128, T], BF16)
    NST = 2 * (2048 + 4096)
    ag_in = dscr("ag_in", [128, NST], F32); ag_out = dscr("ag_out", [NCORES * 128, NST], F32)
    agd_in = dscr("agd_in", [1, 160], F32); agd_out = dscr("agd_out", [NCORES, 160], F32)

    def sbt(name, shape, dt):
        return nc.alloc_sbuf_tensor("sb_" + name, list(shape), dt)

    cstf = sbt("cstf", [128, 8, 128], F32)
    cstb = sbt("cstb", [128, 8, 128], BF16)
    corec = sbt("corec", [128, 2 * 9 * 8 + 2 * 9 + 4], F32)
    modp = sbt("modp", [128, 6, 16, 2], F32)
    w1e = sbt("w1e", [128, 2, 16], F32); n1w = sbt("n1w", [128, 16], F32); n2w = sbt("n2w", [128, 16], F32)
    w2e = sbt("w2e", [128, 16], F32)
    scl = sbt("scl", [128, 64], F32)
    lg = sbt("lg", [128, 32], F32)
    rM = sbt("rM", [128, 16, 128], F32)
    rvec = sbt("rvec", [128, 6, 16], F32)
    abc = sbt("abc", [128, 128], F32); dtbb = sbt("dtbb", [128, 128], F32); ddb = sbt("ddb", [128, 64], F32)
    convw = sbt("convw", [128, 48, 5], F32); convb = sbt("convb", [128, 48], F32)
    AR = Arena(nc, "AR", 188 * 1024)
    psb = [nc.alloc_psum_tensor("psb%d" % i, [128, 512], F32) for i in range(6)]
    pst = [nc.alloc_psum_tensor("pst%d" % i, [128, 1024], BF16) for i in range(2)]
    ps_i = [0]
    nrm_i = [0]
    pt_i = [0]

    def ps_next():
        i = ps_i[0] % 6
        ps_i[0] += 1
        return psb[i], "psb%d" % i

    def pt_next():
        i = pt_i[0] % 2
        pt_i[0] += 1
        return pst[i], "pst%d" % i

    ident = cstb[:, 0, :]

    def dma(eng, out, in_, reads, writes, key):
        P.op(eng, lambda e: e.dma_start(out=out, in_=in_), reads=reads, writes=writes, dma=key)

    def log1p_tile(dst, y, shape, k_in, k_out, tmp):
        (t1, k1), (t2, k2), (t3, k3) = tmp
        P.op("dve", lambda e: e.tensor_scalar(out=t1, in0=y, scalar1=0.05, scalar2=None, op0=ALU.min), reads=[k_in], writes=[k1])
        P.op("dve", lambda e: e.tensor_scalar(out=t2, in0=t1, scalar1=-1.0 / 6, scalar2=0.2, op0=ALU.mult, op1=ALU.add), reads=[k1], writes=[k2])
        for cf in (-0.25, 1.0 / 3, -0.5, 1.0):
            P.op("dve", lambda e: e.tensor_tensor(out=t2, in0=t2, in1=t1, op=ALU.mult), reads=[k1, k2], writes=[k2])
            P.op("dve", lambda e, cf=cf: e.tensor_scalar(out=t2, in0=t2, scalar1=cf, scalar2=None, op0=ALU.add), reads=[k2], writes=[k2])
        P.op("dve", lambda e: e.tensor_tensor(out=t2, in0=t2, in1=t1, op=ALU.mult), reads=[k1, k2], writes=[k2])
        P.op("act", lambda e: e.activation(out=t3, in_=y, func=AF.Ln, bias=1.0, scale=1.0), reads=[k_in], writes=[k3])
        P.op("dve", lambda e: e.tensor_tensor(out=t2, in0=t2, in1=t3, op=ALU.subtract), reads=[k2, k3], writes=[k2])
        P.op("dve", lambda e: e.tensor_scalar(out=t1, in0=y, scalar1=0.05, scalar2=None, op0=ALU.is_lt), reads=[k_in], writes=[k1])
        P.op("dve", lambda e: e.tensor_tensor(out=t2, in0=t2, in1=t1, op=ALU.mult), reads=[k1, k2], writes=[k2])
        P.op("dve", lambda e: e.tensor_tensor(out=dst, in0=t2, in1=t3, op=ALU.add), reads=[k2, k3], writes=[k_out])

    AR.reset()
    dma("sp", cstf[:], cst_d.rearrange("p (a b) -> p a b", a=8), [], ["cstf"], "cstf")
    dma("sp", corec[:], core_d, [], ["corec"], "corec")
    P.op("dve", lambda e: e.tensor_copy(out=cstb[:], in_=cstf[:]), reads=["cstf"], writes=["cstb"])
    P.op("dve", lambda e: e.memset(scl[:], 0.0), writes=["scl"])
    dma("sp", n1w[:], n1w_d, [], ["n1w"], "n1w")
    dma("sp", n2w[:], n2w_d, [], ["n2w"], "n2w")
    dma("sp", convw[:], conv_w_d, [], ["convw"], "convw")
    dma("sp", convb[:], conv_b_d, [], ["convb"], "convb")
    dma("sp", lg[:], rdl_d.partition_broadcast(128), [], ["lg"], "lg")
    dma("sp", abc[:], alog_d.partition_broadcast(128), [], ["abc"], "abc")
    dma("sp", dtbb[:], dtb_d.partition_broadcast(128), [], ["dtbb"], "dtbb")
    dma("sp", ddb[:], ssdd_d.partition_broadcast(128), [], ["ddb"], "ddb")
    P.op("act", lambda e: e.activation(out=abc[:], in_=abc[:], func=AF.Exp), reads=["abc"], writes=["abc"])
    P.op("dve", lambda e: e.tensor_scalar(out=abc[:], in0=abc[:], scalar1=-1.0, scalar2=None, op0=ALU.mult), reads=["abc"], writes=["abc"])
    lgt = [AR.get([32], F32, "lgt") for _ in range(4)]
    P.op("act", lambda e: e.activation(out=lgt[0][0], in_=lg[:], func=AF.Exp, scale=-1.0), reads=["lg"], writes=[lgt[0][1]])
    log1p_tile(lg[:], lgt[0][0], [32], lgt[0][1], "lg", lgt[1:4])
    P.op("dve", lambda e: e.tensor_scalar(out=lg[:], in0=lg[:], scalar1=-1.0, scalar2=None, op0=ALU.mult), reads=["lg"], writes=["lg"])
    midx = cstf[:, 7, 0:1]
    P.op("dve", lambda e: e.tensor_scalar(out=scl[:, 0:1], in0=midx, scalar1=1.0, scalar2=None, op0=ALU.add), reads=["cstf", "scl"], writes=["scl0"])
    P.op("dve", lambda e: e.tensor_scalar(out=scl[:, 1:2], in0=midx, scalar1=-1.0, scalar2=128.0, op0=ALU.mult, op1=ALU.add), reads=["cstf", "scl"], writes=["scl1"])
    P.op("dve", lambda e: e.tensor_scalar(out=scl[:, 2:3], in0=midx, scalar1=-1.0, scalar2=127.0, op0=ALU.mult, op1=ALU.add), reads=["cstf", "scl"], writes=["scl2"])
    for (idx, dr, col, kk) in ((0, 0, 0, "scl0"), (1, 1, 1, "scl1"), (2, 0, 2, "scl2")):
        P.op("dve", lambda e, idx=idx, dr=dr, col=col: e.tensor_scalar(out=rvec[:, idx, :], in0=lg[:, dr * 16:(dr + 1) * 16], scalar1=scl[:, col:col + 1], scalar2=None, op0=ALU.mult), reads=["lg", kk], writes=["rvec%d" % idx])
    P.op("dve", lambda e: e.tensor_scalar(out=rvec[:, 3, :], in0=lg[:, 16:32], scalar1=midx, scalar2=None, op0=ALU.mult), reads=["lg", "cstf"], writes=["rvec3"])
    P.op("dve", lambda e: e.tensor_scalar(out=rvec[:, 4, :], in0=lg[:, 0:16], scalar1=128.0, scalar2=None, op0=ALU.mult), reads=["lg"], writes=["rvec4"])
    P.op("dve", lambda e: e.tensor_scalar(out=rvec[:, 5, :], in0=lg[:, 16:32], scalar1=128.0, scalar2=None, op0=ALU.mult), reads=["lg"], writes=["rvec5"])
    P.op("act", lambda e: e.activation(out=rvec[:], in_=rvec[:], func=AF.Exp), reads=["rvec%d" % i for i in range(6)], writes=["rvec"])
    mt = AR.get([128], F32, "mt")
    for h in range(16):
        P.op("act", lambda e, h=h: e.activation(out=rM[:, h, :], in_=cstf[:, 6, :], func=AF.Exp, scale=lg[:, h:h + 1]), reads=["lg", "cstf"], writes=["rM%d" % h])
        P.op("dve", lambda e, h=h: e.tensor_tensor(out=rM[:, h, :], in0=rM[:, h, :], in1=cstf[:, 1, :], op=ALU.mult), reads=["rM%d" % h, "cstf"], writes=["rM%d" % h])
        P.op("act", lambda e, h=h: e.activation(out=mt[0], in_=cstf[:, 7, :], func=AF.Exp, scale=lg[:, 16 + h:17 + h]), reads=["lg", "cstf"], writes=[mt[1]])
        P.op("dve", lambda e, h=h: e.tensor_tensor(out=mt[0], in0=mt[0], in1=cstf[:, 2, :], op=ALU.mult), reads=[mt[1], "cstf"], writes=[mt[1]])
        P.op("dve", lambda e, h=h: e.tensor_tensor(out=rM[:, h, :], in0=rM[:, h, :], in1=mt[0], op=ALU.add), reads=["rM%d" % h, mt[1]], writes=["rM%d" % h, "rM"])

    cv = AR.get([D], F32, "cv")
    cvb = AR.get([D], BF16, "cvb")
    cT = AR.get([16, 2], BF16, "cT")
    dma("sp", cv[0][0:2, :], cvec_d, [], [cv[1]], "cv")
    P.op("act", lambda e: e.activation(out=cvb[0][0:2, :], in_=cv[0][0:2, :], func=AF.Silu), reads=[cv[1]], writes=[cvb[1]])
    for kc in range(16):
        pt, pk = pt_next()
        P.op("pe", lambda e, kc=kc, pt=pt: e.transpose(out=pt[:, 0:2], in_=cvb[0][0:2, kc * 128:(kc + 1) * 128], identity=cstb[0:2, 0, 0:2]), reads=[cvb[1], "cstb"], writes=[pk])
        P.op("act", lambda e, kc=kc, pt=pt: e.copy(out=cT[0][:, kc, :], in_=pt[:, 0:2]), reads=[pk], writes=[cT[1]])
    bmod = AR.get([96], F32, "bmod")
    dma("sp", bmod[0], b_mod_d, [], [bmod[1]], "bmod")
    wmr = Ring(AR, 2, [16, 512], BF16, "wm")
    for j in range(24):
        wt, wk = wmr.next()
        for q4 in range(4):
            dma("pool", wt[:, q4 * 4:(q4 + 1) * 4, :], w_mod_d[q4 * 512:(q4 + 1) * 512, j * 512:(j + 1) * 512].rearrange("(kc p) n -> p kc n", p=128), [], [wk], wk)
        for cb in range(4):
            ps, pk = ps_next()
            blk = j * 4 + cb

            def mm(e, wt=wt, cb=cb, ps=ps):
                r = None
                for kc in range(16):
                    r = e.matmul(ps[:, 0:2], lhsT=wt[:, kc, cb * 128:(cb + 1) * 128], rhs=cT[0][:, kc, :], start=(kc == 0), stop=(kc == 15))
                return r
            P.op("pe", mm, reads=[wk, cT[1]], writes=[pk])
            P.op("act", lambda e, blk=blk, ps=ps: e.activation(out=modp[:, blk // 16, blk % 16, :], in_=ps[:, 0:2], func=AF.Identity, bias=bmod[0][:, blk:blk + 1], scale=1.0), reads=[pk, bmod[1]], writes=["modp"])
    for s in range(2):
        P.op("dve", lambda e, s=s: e.scalar_tensor_tensor(out=w1e[:, s, :], in0=modp[:, 1, :, s], scalar=1.0, in1=n1w[:], op0=ALU.add, op1=ALU.mult), reads=["modp", "n1w"], writes=["w1e"])
    P.op("dve", lambda e: e.scalar_tensor_tensor(out=w2e[:], in0=modp[:, 4, :, 0], scalar=1.0, in1=n2w[:], op0=ALU.add, op1=ALU.mult), reads=["modp", "n2w"], writes=["w2e"])
    P.barrier()

    AR.reset()
    uT, uTk = AR.get([16, T + 4], BF16, "uT")
    uTc, uTck = AR.get([16, CTX], BF16, "uTc")
    xin_r = Ring(AR, 2, [D], F32, "xin")
    xh_r = Ring(AR, 2, [D], BF16, "xh")
    sq_r = Ring(AR, 1, [D], BF16, "sq")

    def norm_chunk(src_rows, nrows, dstT, dkey, col0, s):
        xin, xk = xin_r.next()
        xh, xhk = xh_r.next()
        sq, sqk = sq_r.next()
        si = nrm_i[0] % 16
        nrm_i[0] += 1
        ssq = scl[:, 16 + si:17 + si]
        P.op("dve", lambda e: e.memset(ssq, 0.0), writes=["ssq%d" % si])
        dma("sp", xin[0:nrows, :], src_rows, [], [xk], xk)
        P.op("act", lambda e: e.activation(out=sq[0:nrows, :], in_=xin[0:nrows, :], func=AF.Square, accum_out=ssq[0:nrows, :]), reads=[xk], writes=[sqk, "ssq%d" % si])
        P.op("dve", lambda e: e.tensor_scalar(out=ssq[0:nrows, :], in0=ssq[0:nrows, :], scalar1=1.0 / D, scalar2=EPS, op0=ALU.mult, op1=ALU.add), reads=["ssq%d" % si], writes=["ssq%d" % si])
        P.op("act", lambda e: e.activation(out=ssq[0:nrows, :], in_=ssq[0:nrows, :], func=AF.Ln), reads=["ssq%d" % si], writes=["ssq%d" % si])
        P.op("act", lambda e: e.activation(out=ssq[0:nrows, :], in_=ssq[0:nrows, :], func=AF.Exp, scale=-0.5), reads=["ssq%d" % si], writes=["ssq%d" % si])
        P.op("dve", lambda e: e.tensor_scalar(out=xh[0:nrows, :], in0=xin[0:nrows, :], scalar1=ssq[0:nrows, :], scalar2=None, op0=ALU.mult), reads=[xk, "ssq%d" % si], writes=[xhk])
        for g in range(2):
            pt, pk = pt_next()

            def tr(e, g=g, pt=pt):
                r = None
                for q in range(8):
                    kc = g * 8 + q
                    r = e.transpose(out=pt[:, q * 128:q * 128 + nrows], in_=xh[0:nrows, kc * 128:(kc + 1) * 128], identity=cstb[0:nrows, 0, 0:nrows])
                return r
            P.op("pe", tr, reads=[xhk, "cstb"], writes=[pk])
            for q in range(8):
                kc = g * 8 + q
                P.op("act", lambda e, kc=kc, q=q, pt=pt: e.activation(out=dstT[:, kc, col0:col0 + nrows], in_=pt[:, q * 128:q * 128 + nrows], func=AF.Identity, scale=w1e[:, s, kc:kc + 1], bias=modp[:, 0, kc, s:s + 1]), reads=[pk, "w1e", "modp"], writes=[dkey])

    for t in range(NCH):
        norm_chunk(x_d[t * 128:(t + 1) * 128, :], 128, uT, uTk, t * 128, 0)
    norm_chunk(xh_d, 4, uT, uTk, T, 0)
    for t in range(2):
        norm_chunk(ctx_d[t * 128:(t + 1) * 128, :], 128, uTc, uTck, t * 128, 1)

    ropet, ropek = AR.get([2, NCH, 64], F32, "rope")
    dma("sp", ropet, rope_d, [], [ropek], "rope")
    wr = Ring(AR, 2, [16, 512], BF16, "wt")
    stg_r = Ring(AR, 3, [512], BF16, "stg")
    stf_r = Ring(AR, 2, [512], F32, "stf")
    rtmp_r = Ring(AR, 2, [512], F32, "rtmp")
    rA_r = Ring(AR, 2, [512], F32, "rA")
    trs_r = Ring(AR, 2, [4, 128], BF16, "trs")

    def load_w(col0, ncols=512):
        wt, wk = wr.next()
        for q4 in range(4):
            dma("pool", wt[:, q4 * 4:(q4 + 1) * 4, 0:ncols], w_in_d[q4 * 512:(q4 + 1) * 512, col0:col0 + ncols].rearrange("(kc p) n -> p kc n", p=128), [], [wk], wk)
        return wt, wk

    def mm_a(ps, pk, AT, ATk, tcol, wt, wk, ncols=512, kcs=16):
        def f(e):
            r = None
            for kc in range(kcs):
                r = e.matmul(ps[:, 0:ncols], lhsT=AT[:, kc, tcol:tcol + 128], rhs=wt[:, kc, 0:ncols], start=(kc == 0), stop=(kc == kcs - 1))
            return r
        P.op("pe", f, reads=[ATk, wk], writes=[pk])

    def evac_plain(ps, pk, dst_rows, ncols=512, fp32=False):
        if fp32:
            st, sk = stf_r.next()
        else:
            st, sk = stg_r.next()
        P.op("act", lambda e: e.copy(out=st[:, 0:ncols], in_=ps[:, 0:ncols]), reads=[pk], writes=[sk])
        dma("sp", dst_rows, st[:, 0:ncols], [sk], [], sk)

    def evac_qk(ps, pk, t, j, is_k, rope, tok_dst, T_dst):
        st, sk = stg_r.next()
        scale = (128.0 ** -0.5) if is_k else 1.0
        if rope:
            A, Ak = rA_r.next()
            tm, tmk = rtmp_r.next()
            x5 = ps[:, :].rearrange("p (h a b f) -> p h a b f", h=4, a=2, b=2)
            A5 = A.rearrange("p (h a b f) -> p h a b f", h=4, a=2, b=2)
            t5 = tm.rearrange("p (h a b f) -> p h a b f", h=4, a=2, b=2)
            cosb = ropet[:, 0, t, :].rearrange("p (a f) -> p a f", a=2)
            sinb = ropet[:, 1, t, :].rearrange("p (a f) -> p a f", a=2)
            sin4 = bc(sinb.unsqueeze(1), [128, 4, 2, 32])
            for aa in range(2):
                P.op("dve", lambda e, aa=aa: e.tensor_tensor(out=A5[:, :, aa, :, :], in0=x5[:, :, aa, :, :], in1=bc(cosb[:, aa, :].unsqueeze(1).unsqueeze(2), [128, 4, 2, 32]), op=ALU.mult), reads=[pk, ropek], writes=[Ak])
            P.op("dve", lambda e: e.tensor_tensor(out=t5[:, :, :, 0, :], in0=x5[:, :, :, 1, :], in1=sin4, op=ALU.mult), reads=[pk, ropek], writes=[tmk])
            P.op("dve", lambda e: e.tensor_tensor(out=t5[:, :, :, 1, :], in0=x5[:, :, :, 0, :], in1=sin4, op=ALU.mult), reads=[pk, ropek], writes=[tmk])
            P.op("dve", lambda e: e.tensor_tensor(out=A5[:, :, :, 0, :], in0=A5[:, :, :, 0, :], in1=t5[:, :, :, 0, :], op=ALU.subtract), reads=[Ak, tmk], writes=[Ak])
            P.op("dve", lambda e: e.tensor_tensor(out=A5[:, :, :, 1, :], in0=A5[:, :, :, 1, :], in1=t5[:, :, :, 1, :], op=ALU.add), reads=[Ak, tmk], writes=[Ak])
            P.op("act", lambda e: e.activation(out=st[:, :], in_=A[:, :], func=AF.Copy, scale=scale), reads=[Ak], writes=[sk])
        else:
            P.op("act", lambda e: e.activation(out=st[:, :], in_=ps[:, :], func=AF.Copy, scale=scale), reads=[pk], writes=[sk])
        if tok_dst is not None:
            dma("sp", tok_dst, st[:, :], [sk], [], sk)
        if T_dst is not None:
            pt, ptk = pt_next()
            ts_, tsk = trs_r.next()

            def tr(e):
                r = None
                for hh in range(4):
                    r = e.transpose(out=pt[:, hh * 128:(hh + 1) * 128], in_=st[:, hh * 128:(hh + 1) * 128], identity=ident)
                return r
            P.op("pe", tr, reads=[sk, "cstb"], writes=[ptk])
            P.op("act", lambda e: e.copy(out=ts_.rearrange("p a b -> p (a b)"), in_=pt[:, 0:512]), reads=[ptk], writes=[tsk])
            dma("sp", T_dst, ts_, [tsk], [], tsk)

    for j in range(4):
        wt, wk = load_w(C_Q + j * 512)
        for t in range(NCH):
            ps, pk = ps_next()
            mm_a(ps, pk, uT, uTk, t * 128, wt, wk)
            evac_qk(ps, pk, t, j, False, True, None, qT_s[t, :, j * 4:(j + 1) * 4, :])
    for j in range(4):
        wt, wk = load_w(C_K + j * 512)
        for t in range(NCH):
            ps, pk = ps_next()
            mm_a(ps, pk, uT, uTk, t * 128, wt, wk)
            evac_qk(ps, pk, t, j, True, True, ktok_s[t * 128:(t + 1) * 128, j * 512:(j + 1) * 512], kT_s[t, :, j * 4:(j + 1) * 4, :])
        for t in range(2):
            ps, pk = ps_next()
            mm_a(ps, pk, uTc, uTck, t * 128, wt, wk)
            evac_qk(ps, pk, t, j, True, False, cktok_s[t * 128:(t + 1) * 128, j * 512:(j + 1) * 512], None)
    for (c0, n512, dst, cdst) in ((C_V, 4, v_s, cv_s), (C_G, 4, g_s, None), (C_Z, 8, z_s, None), (C_BL, 8, bl_s, None)):
        for j in range(n512):
            wt, wk = load_w(c0 + j * 512)
            for t in range(NCH):
                ps, pk = ps_next()
                mm_a(ps, pk, uT, uTk, t * 128, wt, wk)
                evac_plain(ps, pk, dst[t * 128:(t + 1) * 128, j * 512:(j + 1) * 512])
            if cdst is not None:
                for t in range(2):
                    ps, pk = ps_next()
                    mm_a(ps, pk, uTc, uTck, t * 128, wt, wk)
                    evac_plain(ps, pk, cdst[t * 128:(t + 1) * 128, j * 512:(j + 1) * 512])
    wt, wk = load_w(C_DT, 128)
    for t in range(NCH):
        ps, pk = ps_next()
        mm_a(ps, pk, uT, uTk, t * 128, wt, wk, ncols=128)
        evac_plain(ps, pk, dt_s[t * 128:(t + 1) * 128, :], ncols=128, fp32=True)
    for t in range(2):
        ps, pk = ps_next()
        mm_a(ps, pk, uTc, uTck, t * 128, wt, wk, ncols=128)
        evac_plain(ps, pk, cdt_s[t * 128:(t + 1) * 128, :], ncols=128, fp32=True)

    xraw, xrk = AR.get([T + 4], F32, "xraw")
    cacc, cak = AR.get([T], F32, "cacc")
    xsil_r = Ring(AR, 2, [T], BF16, "xsil")
    tks_r = Ring(AR, 2, [8, 128], BF16, "tks")
    xrawc, xrck = AR.get([CTX + 4], F32, "xrawc")

    def conv_block(blk, raw, rawk, n, dstT, tok_dst, col0):
        xs_, xsk = xsil_r.next()
        P.op("dve", lambda e: e.tensor_scalar(out=cacc[:, 0:n], in0=raw[:, 0:n], scalar1=convw[:, blk, 0:1], scalar2=convb[:, blk:blk + 1], op0=ALU.mult, op1=ALU.add), reads=[rawk, "convw", "convb"], writes=[cak])
        for jj in range(1, 5):
            P.op("dve", lambda e, jj=jj: e.scalar_tensor_tensor(out=cacc[:, 0:n], in0=raw[:, jj:jj + n], scalar=convw[:, blk, jj:jj + 1], in1=cacc[:, 0:n], op0=ALU.mult, op1=ALU.add), reads=[rawk, "convw", cak], writes=[cak])
        P.op("act", lambda e: e.activation(out=xs_[:, 0:n], in_=cacc[:, 0:n], func=AF.Silu), reads=[cak], writes=[xsk])
        if dstT is not None:
            dma("sp", dstT, xs_[:, 0:n].rearrange("p (t i) -> p t i", i=128), [xsk], [], xsk)
        if tok_dst is not None:
            for g in range(n // 1024 if n >= 1024 else 1):
                nt = min(8, n // 128)
                pt, ptk = pt_next()
                ts_, tsk = tks_r.next()

                def tr(e, g=g, pt=pt, nt=nt):
                    r = None
                    for q in range(nt):
                        tt = g * 8 + q
                        r = e.transpose(out=pt[:, q * 128:(q + 1) * 128], in_=xs_[:, tt * 128:(tt + 1) * 128], identity=ident)
                    return r
                P.op("pe", tr, reads=[xsk, "cstb"], writes=[ptk])
                P.op("act", lambda e, pt=pt, ts_=ts_, nt=nt: e.copy(out=ts_[:, 0:nt, :].rearrange("p a b -> p (a b)"), in_=pt[:, 0:nt * 128]), reads=[ptk], writes=[tsk])
                dma("sp", tok_dst[g * 1024:g * 1024 + nt * 128, col0:col0 + 128].rearrange("(q i) c -> i q c", i=128), ts_[:, 0:nt, :], [tsk], [], tsk)

    for j in range(12):
        wt, wk = load_w(C_XBC + j * 512)
        for cb in range(4):
            blk = j * 4 + cb
            for tt in range(4):
                ps, pk = ps_next()

                def f(e, ps=ps, cb=cb, tt=tt, wt=wt):
                    r = None
                    for kc in range(16):
                        r = e.matmul(ps[:, :], lhsT=wt[:, kc, cb * 128:(cb + 1) * 128], rhs=uT[:, kc, tt * 512:(tt + 1) * 512], start=(kc == 0), stop=(kc == 15))
                    return r
                P.op("pe", f, reads=[uTk, wk], writes=[pk])
                P.op("act", lambda e, ps=ps, tt=tt: e.copy(out=xraw[:, 2 + tt * 512:2 + (tt + 1) * 512], in_=ps[:, :]), reads=[pk], writes=[xrk])
            ps, pk = ps_next()

            def fh(e, ps=ps, cb=cb, wt=wt):
                r = None
                for kc in range(16):
                    r = e.matmul(ps[:, 0:4], lhsT=wt[:, kc, cb * 128:(cb + 1) * 128], rhs=uT[:, kc, T:T + 4], start=(kc == 0), stop=(kc == 15))
                return r
            P.op("pe", fh, reads=[uTk, wk], writes=[pk])
            NCC = 2 * 9 * 8 + 2 * 9
            P.op("dve", lambda e, ps=ps: e.tensor_scalar(out=xraw[:, 0:2], in0=ps[:, 0:2], scalar1=corec[:, NCC:NCC + 1], scalar2=None, op0=ALU.mult), reads=[pk, "corec"], writes=[xrk])
            P.op("dve", lambda e, ps=ps: e.tensor_scalar(out=xraw[:, T + 2:T + 4], in0=ps[:, 2:4], scalar1=corec[:, NCC + 1:NCC + 2], scalar2=None, op0=ALU.mult), reads=[pk, "corec"], writes=[xrk])
            if blk < 32:
                conv_block(blk, xraw, xrk, T, None, xs_s, blk * 128)
            elif blk < 40:
                conv_block(blk, xraw, xrk, T, bT_s[:, :, blk - 32, :].rearrange("t p i -> p t i"), btok_s, (blk - 32) * 128)
            else:
                conv_block(blk, xraw, xrk, T, cT_s[:, :, blk - 40, :].rearrange("t p i -> p t i"), None, 0)
            if blk < 40:
                ps, pk = ps_next()

                def fc(e, ps=ps, cb=cb, wt=wt):
                    r = None
                    for kc in range(16):
                        r = e.matmul(ps[:, 0:CTX], lhsT=wt[:, kc, cb * 128:(cb + 1) * 128], rhs=uTc[:, kc, :], start=(kc == 0), stop=(kc == 15))
                    return r
                P.op("pe", fc, reads=[uTck, wk], writes=[pk])
                P.op("dve", lambda e: e.memset(xrawc[:, :], 0.0), writes=[xrck])
                P.op("act", lambda e, ps=ps: e.copy(out=xrawc[:, 2:2 + CTX], in_=ps[:, 0:CTX]), reads=[pk], writes=[xrck])
                if blk < 32:
                    conv_block(blk, xrawc, xrck, CTX, None, cxs_s, blk * 128)
                else:
                    conv_block(blk, xrawc, xrck, CTX, None, cbtok_s, (blk - 32) * 128)
    P.barrier()

    AR.reset()
    Hr = AR.get([2, 16, 128], F32, "Hr")[0]; Hs = AR.get([2, 64, 64], F32, "Hs")[0]
    Hrb = AR.get([2, 16, 128], BF16, "Hrb")[0]; Hsb = AR.get([2, 64, 64], BF16, "Hsb")[0]
    hc_s = dscr("hc_s", [128, NST], F32)
    mix_mark = AR.off
    ktok_r = Ring(AR, 1, [16, 128], BF16, "ktok"); v_r = Ring(AR, 1, [16, 128], BF16, "v")
    qT_r = Ring(AR, 1, [16, 128], BF16, "qT"); kT_r = Ring(AR, 1, [16, 128], BF16, "kT")
    xs_r = Ring(AR, 1, [64, 64], BF16, "xs"); bt_r = Ring(AR, 1, [8, 128], BF16, "btok")
    bT_r = Ring(AR, 1, [8, 128], BF16, "bT"); cT_r = Ring(AR, 1, [8, 128], BF16, "cT")
    dt_r = Ring(AR, 2, [128], F32, "dtr")
    ksc, ksck = AR.get([16, 128], BF16, "ksc")
    dtt, dttk = AR.get([128], F32, "dtt")
    l1tmp = [AR.get([128], F32, "l1t") for _ in range(4)]
    la, lak = AR.get([128], F32, "la")
    lahl, lahlk = AR.get([2, 128], BF16, "lahl")
    cumt, cumk = AR.get([4, 64], F32, "cum")
    cdec, cdeck = AR.get([64], F32, "cdec")
    vst, vstk = AR.get([64, 64], BF16, "vst")
    vin, vink = AR.get([2, 64, 64], BF16, "vin")
    SM, SMk = AR.get([2, 8, 128], F32, "SM")
    X_r = Ring(AR, 2, [2, 4, 128], BF16, "X")
    eE_r = Ring(AR, 2, [4, 128], BF16, "eE")
    W_r = Ring(AR, 4, [4, 128], BF16, "W")
    Pt_r = Ring(AR, 2, [4, 128], BF16, "Pt")
    yacc, yacck = AR.get([6144], F32, "yacc")
    ytmp_r = Ring(AR, 2, [512], F32, "ytmp")

    def zero_states():
        P.op("dve", lambda e: e.memset(Hr, 0.0), writes=["Hr0", "Hr1"])
        P.op("dve", lambda e: e.memset(Hs, 0.0), writes=["Hs0", "Hs1"])
        P.op("dve", lambda e: e.memset(Hrb, 0.0), writes=["Hrb0", "Hrb1"])
        P.op("dve", lambda e: e.memset(Hsb, 0.0), writes=["Hsb0", "Hsb1"])

    def scan_chunk(t, d, src, with_out, intra):
        r0, r1 = t * 128, (t + 1) * 128
        kt, ktk = ktok_r.next(); vv, vk = v_r.next(); xs_, xsk = xs_r.next(); bt, btk = bt_r.next(); dr, drk = dt_r.next()
        dma("sp", kt.rearrange("p a b -> p (a b)"), src["ktok"][r0:r1, :], [], [ktk], ktk)
        dma("sp", vv.rearrange("p a b -> p (a b)"), src["v"][r0:r1, :], [], [vk], vk)
        dma("sp", xs_.rearrange("p a b -> p (a b)"), src["xs"][r0:r1, :], [], [xsk], xsk)
        dma("sp", bt.rearrange("p a b -> p (a b)"), src["btok"][r0:r1, :], [], [btk], btk)
        dma("sp", dr, src["dt"][r0:r1, :], [], [drk], drk)
        if with_out:
            qT, qTk = qT_r.next(); kT, kTk = kT_r.next(); bT, bTk = bT_r.next(); cT_, cTk = cT_r.next()
            dma("sp", qT, qT_s[t], [], [qTk], qTk)
            dma("sp", cT_, cT_s[t], [], [cTk], cTk)
            if intra:
                dma("sp", kT, kT_s[t], [], [kTk], kTk)
                dma("sp", bT, bT_s[t], [], [bTk], bTk)
        P.op("dve", lambda e: e.tensor_tensor(out=dtt, in0=dr, in1=dtbb[:], op=ALU.add), reads=[drk, "dtbb"], writes=[dttk])
        (ta, tak) = l1tmp[0]
        P.op("act", lambda e: e.activation(out=ta, in_=dtt, func=AF.Abs), reads=[dttk], writes=[tak])
        P.op("act", lambda e: e.activation(out=ta, in_=ta, func=AF.Exp, scale=-1.0), reads=[tak], writes=[tak])
        log1p_tile(la, ta, [128], tak, lak, l1tmp[1:4])
        P.op("dve", lambda e: e.scalar_tensor_tensor(out=dtt, in0=dtt, scalar=0.0, in1=la, op0=ALU.max, op1=ALU.add), reads=[dttk, lak], writes=[dttk])
        P.op("dve", lambda e: e.tensor_tensor(out=la, in0=dtt, in1=abc[:], op=ALU.mult), reads=[dttk, "abc"], writes=[lak])
        P.op("dve", lambda e: e.tensor_copy(out=lahl[:, 0, :], in_=la), reads=[lak], writes=[lahlk])
        P.op("dve", lambda e: e.tensor_tensor(out=ta, in0=la, in1=lahl[:, 0, :], op=ALU.subtract), reads=[lak, lahlk], writes=[tak])
        P.op("dve", lambda e: e.tensor_copy(out=lahl[:, 1, :], in_=ta), reads=[tak], writes=[lahlk])

        def ssd_dir(dd):
            Tm = cstb[:, 1 if dd == 0 else 3, :]
            ps, pk = ps_next()

            def f(e):
                e.matmul(ps[:, 0:64], lhsT=Tm, rhs=lahl[:, 0, dd * 64:(dd + 1) * 64], start=True, stop=False)
                e.matmul(ps[:, 0:64], lhsT=Tm, rhs=lahl[:, 1, dd * 64:(dd + 1) * 64], start=False, stop=True)
                e.matmul(ps[:, 64:128], lhsT=cstb[:, 5, :], rhs=lahl[:, 0, dd * 64:(dd + 1) * 64], start=True, stop=False)
                return e.matmul(ps[:, 64:128], lhsT=cstb[:, 5, :], rhs=lahl[:, 1, dd * 64:(dd + 1) * 64], start=False, stop=True)
            P.op("pe", f, reads=[lahlk, "cstb"], writes=[pk])
            P.op("act", lambda e: e.copy(out=cumt[:, 0:2, :].rearrange("p a b -> p (a b)"), in_=ps[:, 0:128]), reads=[pk], writes=[cumk])
            P.op("act", lambda e: e.activation(out=cumt[:, 2, :], in_=cumt[:, 0, :], func=AF.Exp), reads=[cumk], writes=[cumk + "s"])
            P.op("dve", lambda e: e.tensor_tensor(out=cumt[:, 3, :], in0=cumt[:, 1, :], in1=cumt[:, 0, :], op=ALU.subtract), reads=[cumk], writes=[cumk + "e"])
            P.op("act", lambda e: e.activation(out=cumt[:, 3, :], in_=cumt[:, 3, :], func=AF.Exp), reads=[cumk + "e"], writes=[cumk + "e"])
            P.op("act", lambda e: e.activation(out=cdec, in_=cumt[:, 1, :], func=AF.Exp), reads=[cumk], writes=[cdeck])

        if with_out:
            if intra:
                P.op("dve", lambda e: e.memset(yacc, 0.0), writes=[yacck])
            else:
                dma("sp", yacc, y_s[r0:r1, :], ["ys%d" % t], [yacck], yacck)
            for hg in range(4):
                if intra:
                    ps, pk = ps_next()

                    def fs(e, ps=ps, hg=hg):
                        r = None
                        for hh in range(4):
                            h = hg * 4 + hh
                            r = e.matmul(ps[:, hh * 128:(hh + 1) * 128], lhsT=kT[:, h, :], rhs=qT[:, h, :], start=True, stop=True)
                        return r
                    P.op("pe", fs, reads=[kTk, qTk], writes=[pk])
                    Pt, Ptk = Pt_r.next()
                    P.op("dve", lambda e, ps=ps, hg=hg, Pt=Pt: e.tensor_tensor(out=Pt, in0=ps[:, :].rearrange("p (a b) -> p a b", a=4), in1=rM[:, hg * 4:(hg + 1) * 4, :], op=ALU.mult), reads=[pk, "rM"], writes=[Ptk])
                    ps2, pk2 = ps_next()

                    def fi(e, ps2=ps2, hg=hg, Pt=Pt):
                        r = None
                        for hh in range(4):
                            h = hg * 4 + hh
                            r = e.matmul(ps2[:, hh * 128:(hh + 1) * 128], lhsT=Pt[:, hh, :], rhs=vv[:, h, :], start=True, stop=True)
                        return r
                    P.op("pe", fi, reads=[Ptk, vk], writes=[pk2])
                    P.op("act", lambda e, ps2=ps2, hg=hg: e.copy(out=yacc[:, hg * 512:(hg + 1) * 512], in_=ps2[:, :]), reads=[pk2, yacck], writes=[yacck])
                ps3, pk3 = ps_next()

                def fc(e, ps3=ps3, hg=hg):
                    r = None
                    for hh in range(4):
                        h = hg * 4 + hh
                        r = e.matmul(ps3[:, hh * 128:(hh + 1) * 128], lhsT=qT[:, h, :], rhs=Hrb[:, d, h, :], start=True, stop=True)
                    return r
                P.op("pe", fc, reads=[qTk, "Hrb%d" % d], writes=[pk3])
                for hh in range(4):
                    h = hg * 4 + hh
                    P.op("dve", lambda e, ps3=ps3, hh=hh, h=h: e.scalar_tensor_tensor(out=yacc[:, h * 128:(h + 1) * 128], in0=ps3[:, hh * 128:(hh + 1) * 128], scalar=rvec[:, d, h:h + 1], in1=yacc[:, h * 128:(h + 1) * 128], op0=ALU.mult, op1=ALU.add), reads=[pk3, "rvec", yacck], writes=[yacck])
        dirs = [0, 1] if (with_out and intra) else [d]
        for dd in dirs:
            if with_out and intra:
                pass
        if with_out and intra:
            for g2 in range(2):
                ps, pk = ps_next()

                def fsc(e, ps=ps, g2=g2):
                    r = None
                    for gg in range(4):
                        g = g2 * 4 + gg
                        r = e.matmul(ps[:, gg * 128:(gg + 1) * 128], lhsT=bT[:, g, :], rhs=cT_[:, g, :], start=True, stop=True)
                    return r
                P.op("pe", fsc, reads=[bTk, cTk], writes=[pk])
                P.op("dve", lambda e, ps=ps, g2=g2: e.tensor_tensor(out=SM[:, 0, g2 * 4:(g2 + 1) * 4, :], in0=ps[:, :].rearrange("p (a b) -> p a b", a=4), in1=bc(cstf[:, 1, :].unsqueeze(1), [128, 4, 128]), op=ALU.mult), reads=[pk, "cstf"], writes=[SMk])
                P.op("dve", lambda e, ps=ps, g2=g2: e.tensor_tensor(out=SM[:, 1, g2 * 4:(g2 + 1) * 4, :], in0=ps[:, :].rearrange("p (a b) -> p a b", a=4), in1=bc(cstf[:, 2, :].unsqueeze(1), [128, 4, 128]), op=ALU.mult), reads=[pk, "cstf"], writes=[SMk])
            for dd in range(2):
                P.op("dve", lambda e, dd=dd: e.tensor_tensor(out=vin[:, dd, :, :], in0=xs_, in1=bc(dtt[:, dd * 64:(dd + 1) * 64].unsqueeze(2), [128, 64, 64]), op=ALU.mult), reads=[xsk, dttk], writes=[vink])
            for g in range(8):
                psy, pky = ps_next()
                for h4 in range(2):
                    h0 = g * 8 + h4 * 4
                    Ws = []
                    for dd in range(2):
                        Tm = cstf[:, 1 if dd == 0 else 3, :]
                        Lm = cstb[:, 2 if dd == 0 else 4, :]
                        X, Xk = X_r.next()
                        for hl in range(2):
                            P.op("dve", lambda e, X=X, hl=hl, h0=h0, dd=dd, Tm=Tm: e.tensor_tensor(out=X[:, hl, :, :], in0=bc(lahl[:, hl, dd * 64 + h0:dd * 64 + h0 + 4].unsqueeze(2), [128, 4, 128]), in1=bc(Tm.unsqueeze(1), [128, 4, 128]), op=ALU.mult), reads=[lahlk, "cstf"], writes=[Xk])
                        ps, pk = ps_next()

                        def fe(e, ps=ps, X=X, Lm=Lm):
                            e.matmul(ps[:, :], lhsT=Lm, rhs=X[:, 0, :, :].rearrange("p a b -> p (a b)"), start=True, stop=False)
                            return e.matmul(ps[:, :], lhsT=Lm, rhs=X[:, 1, :, :].rearrange("p a b -> p (a b)"), start=False, stop=True)
                        P.op("pe", fe, reads=[Xk, "cstb"], writes=[pk])
                        eE, eEk = eE_r.next()
                        P.op("act", lambda e, ps=ps, eE=eE: e.activation(out=eE.rearrange("p a b -> p (a b)"), in_=ps[:, :], func=AF.Exp), reads=[pk], writes=[eEk])
                        W, Wk = W_r.next()
                        P.op("dve", lambda e, eE=eE, W=W, dd=dd, g=g: e.tensor_tensor(out=W, in0=eE, in1=bc(SM[:, dd, g, :].unsqueeze(1), [128, 4, 128]), op=ALU.mult), reads=[eEk, SMk], writes=[Wk])
                        Ws.append((W, Wk))

                    def fy(e, Ws=Ws, h4=h4, g=g, psy=psy):
                        r = None
                        for hh in range(4):
                            hi = h4 * 4 + hh
                            h = g * 8 + hi
                            e.matmul(psy[:, hi * 64:(hi + 1) * 64], lhsT=Ws[0][0][:, hh, :], rhs=vin[:, 0, h, :], start=True, stop=False)
                            r = e.matmul(psy[:, hi * 64:(hi + 1) * 64], lhsT=Ws[1][0][:, hh, :], rhs=vin[:, 1, h, :], start=False, stop=True)
                        return r
                    P.op("pe", fy, reads=[Ws[0][1], Ws[1][1], vink], writes=[pky])
                P.op("act", lambda e, psy=psy, g=g: e.copy(out=yacc[:, 2048 + g * 512:2048 + (g + 1) * 512], in_=psy[:, :]), reads=[pky, yacck], writes=[yacck])
        ssd_dir(d)
        if with_out:
            for g in range(8):
                ps, pk = ps_next()
                P.op("pe", lambda e, ps=ps, g=g: e.matmul(ps[:, :], lhsT=cT_[:, g, :], rhs=Hsb[:, d, g * 8:(g + 1) * 8, :].rearrange("p a b -> p (a b)"), start=True, stop=True), reads=[cTk, "Hsb%d" % d], writes=[pk])
                yt, ytk = ytmp_r.next()
                P.op("dve", lambda e, ps=ps, g=g, yt=yt: e.tensor_tensor(out=yt.rearrange("p (a b) -> p a b", a=8), in0=ps[:, :].rearrange("p (a b) -> p a b", a=8), in1=bc(cumt[:, 2, g * 8:(g + 1) * 8].unsqueeze(2), [128, 8, 64]), op=ALU.mult), reads=[pk, cumk + "s"], writes=[ytk])
                P.op("dve", lambda e, g=g, yt=yt: e.tensor_tensor(out=yacc[:, 2048 + g * 512:2048 + (g + 1) * 512], in0=yacc[:, 2048 + g * 512:2048 + (g + 1) * 512], in1=yt, op=ALU.add), reads=[ytk, yacck], writes=[yacck])
            dma("sp", y_s[r0:r1, :], yacc, [yacck], ["ys%d" % t], yacck)
        P.op("dve", lambda e: e.tensor_tensor(out=ksc, in0=kt, in1=bc(rvec[:, 2 + d, :].unsqueeze(2), [128, 16, 128]), op=ALU.mult), reads=[ktk, "rvec"], writes=[ksck])
        P.op("dve", lambda e: e.tensor_tensor(out=Hr[:, d, :, :], in0=Hr[:, d, :, :], in1=bc(rvec[:, 4 + d, :].unsqueeze(2), [128, 16, 128]), op=ALU.mult), reads=["Hr%d" % d, "rvec"], writes=["Hr%d" % d])
        for hg in range(4):
            ps, pk = ps_next()

            def fst(e, ps=ps, hg=hg):
                r = None
                for hh in range(4):
                    h = hg * 4 + hh
                    r = e.matmul(ps[:, hh * 128:(hh + 1) * 128], lhsT=ksc[:, h, :], rhs=vv[:, h, :], start=True, stop=True)
                return r
            P.op("pe", fst, reads=[ksck, vk], writes=[pk])
            P.op("dve", lambda e, ps=ps, hg=hg: e.tensor_tensor(out=Hr[:, d, hg * 4:(hg + 1) * 4, :], in0=Hr[:, d, hg * 4:(hg + 1) * 4, :], in1=ps[:, :].rearrange("p (a b) -> p a b", a=4), op=ALU.add), reads=[pk, "Hr%d" % d], writes=["Hr%d" % d])
        P.op("act", lambda e: e.copy(out=Hrb[:, d, :, :], in_=Hr[:, d, :, :]), reads=["Hr%d" % d], writes=["Hrb%d" % d])
        P.op("dve", lambda e: e.tensor_tensor(out=cumt[:, 3, :], in0=cumt[:, 3, :], in1=dtt[:, d * 64:(d + 1) * 64], op=ALU.mult), reads=[cumk + "e", dttk], writes=[cumk + "e"])
        P.op("dve", lambda e: e.tensor_tensor(out=vst, in0=xs_, in1=bc(cumt[:, 3, :].unsqueeze(2), [128, 64, 64]), op=ALU.mult), reads=[xsk, cumk + "e"], writes=[vstk])
        P.op("dve", lambda e: e.tensor_tensor(out=Hs[:, d, :, :], in0=Hs[:, d, :, :], in1=bc(cdec.unsqueeze(2), [128, 64, 64]), op=ALU.mult), reads=["Hs%d" % d, cdeck], writes=["Hs%d" % d])
        for g in range(8):
            ps, pk = ps_next()
            P.op("pe", lambda e, ps=ps, g=g: e.matmul(ps[:, :], lhsT=bt[:, g, :], rhs=vst[:, g * 8:(g + 1) * 8, :].rearrange("p a b -> p (a b)"), start=True, stop=True), reads=[btk, vstk], writes=[pk])
            P.op("dve", lambda e, ps=ps, g=g: e.tensor_tensor(out=Hs[:, d, g * 8:(g + 1) * 8, :], in0=Hs[:, d, g * 8:(g + 1) * 8, :], in1=ps[:, :].rearrange("p (a b) -> p a b", a=8), op=ALU.add), reads=[pk, "Hs%d" % d], writes=["Hs%d" % d])
        P.op("act", lambda e: e.copy(out=Hsb[:, d, :, :], in_=Hs[:, d, :, :]), reads=["Hs%d" % d], writes=["Hsb%d" % d])
        return

    csrc = dict(ktok=cktok_s, v=cv_s, xs=cxs_s, btok=cbtok_s, dt=cdt_s)
    lsrc = dict(ktok=ktok_s, v=v_s, xs=xs_s, btok=btok_s, dt=dt_s)
    zero_states()
    for t in range(2):
        scan_chunk(t, 0, csrc, False, False)
    for t in (1, 0):
        scan_chunk(t, 1, csrc, False, False)
    for d in range(2):
        dma("sp", hc_s[:, d * 6144:d * 6144 + 2048], Hr[:, d, :, :].rearrange("p a b -> p (a b)"), ["Hr%d" % d], ["hc_s"], "hcst")
        dma("sp", hc_s[:, d * 6144 + 2048:(d + 1) * 6144], Hs[:, d, :, :].rearrange("p a b -> p (a b)"), ["Hs%d" % d], ["hc_s"], "hcst")
    zero_states()
    ldec = nc.alloc_sbuf_tensor("sb_ldec", [128, 160], F32)[:]
    ldeck = "ldec"
    P.op("dve", lambda e: e.memset(ldec, 0.0), writes=[ldeck])
    for d, order in ((0, range(NCH)), (1, range(NCH - 1, -1, -1))):
        for t in order:
            scan_chunk(t, d, lsrc, False, False)
            P.op("dve", lambda e, d=d: e.tensor_tensor(out=ldec[:, d * 80 + 16:d * 80 + 80], in0=ldec[:, d * 80 + 16:d * 80 + 80], in1=cumt[:, 1, :], op=ALU.add), reads=[cumk, ldeck], writes=[ldeck])
        P.op("dve", lambda e, d=d: e.tensor_scalar(out=ldec[:, d * 80:d * 80 + 16], in0=lg[:, d * 16:(d + 1) * 16], scalar1=float(T), scalar2=None, op0=ALU.mult), reads=["lg", ldeck], writes=[ldeck])
    for d in range(2):
        dma("sp", ag_in[:, d * 6144:d * 6144 + 2048], Hr[:, d, :, :].rearrange("p a b -> p (a b)"), ["Hr%d" % d], ["ag_in"], "agin")
        dma("sp", ag_in[:, d * 6144 + 2048:(d + 1) * 6144], Hs[:, d, :, :].rearrange("p a b -> p (a b)"), ["Hs%d" % d], ["ag_in"], "agin")
    dma("sp", agd_in, ldec[0:1, :], [ldeck], ["agd_in"], "agin")
    P.barrier()
    P.op("pool", lambda e: e.collective_compute("AllGather", ALU.bypass, replica_groups=[list(range(NCORES))], ins=[ag_in], outs=[ag_out]), reads=["ag_in"], writes=["ag_out"])
    P.op("pool", lambda e: e.collective_compute("AllGather", ALU.bypass, replica_groups=[list(range(NCORES))], ins=[agd_in], outs=[agd_out]), reads=["agd_in"], writes=["agd_out"])
    mix_end = AR.off
    AR.off = mix_mark
    LD, LDk = AR.get([8, 160], F32, "LD")
    dma("sp", LD.rearrange("p a b -> p (a b)"), agd_out.rearrange("r f -> (r f)").partition_broadcast(128), ["agd_out"], [LDk], "LD")
    Wt, Wtk = AR.get([2, 9, 80], F32, "Wt")
    P.op("dve", lambda e: e.memset(Wt, 0.0), writes=[Wtk])
    for d in range(2):
        for s in range(9):
            for c2 in range(8):
                ci = (d * 9 + s) * 8 + c2
                P.op("dve", lambda e, d=d, s=s, c2=c2, ci=ci: e.scalar_tensor_tensor(out=Wt[:, d, s, :], in0=LD[:, c2, d * 80:(d + 1) * 80], scalar=corec[:, ci:ci + 1], in1=Wt[:, d, s, :], op0=ALU.mult, op1=ALU.add), reads=[LDk, "corec", Wtk], writes=[Wtk])
    P.op("act", lambda e: e.activation(out=Wt, in_=Wt, func=AF.Exp), reads=[Wtk], writes=[Wtk])
    for d in range(2):
        for s in range(9):
            vi = 2 * 9 * 8 + d * 9 + s
            P.op("dve", lambda e, d=d, s=s, vi=vi: e.tensor_scalar(out=Wt[:, d, s, :], in0=Wt[:, d, s, :], scalar1=corec[:, vi:vi + 1], scalar2=None, op0=ALU.mult), reads=[Wtk, "corec"], writes=[Wtk])
    P.op("dve", lambda e: e.memset(Hr, 0.0), reads=["Hr0", "Hr1"], writes=["Hr0", "Hr1"])
    P.op("dve", lambda e: e.memset(Hs, 0.0), reads=["Hs0", "Hs1"], writes=["Hs0", "Hs1"])
    sst_r = Ring(AR, 2, [2048], F32, "sst")
    for d in range(2):
        for s in range(9):
            for pc in range(3):
                st, sk = sst_r.next()
                srcd = ag_out[s * 128:(s + 1) * 128, :] if s < 8 else hc_s
                dma("sp", st, srcd[:, d * 6144 + pc * 2048:d * 6144 + (pc + 1) * 2048], ["ag_out", "hc_s"], [sk], sk)
                if pc == 0:
                    s3 = st.rearrange("p (a b) -> p a b", a=16)
                    P.op("dve", lambda e, d=d, s=s, s3=s3: e.tensor_tensor(out=s3, in0=s3, in1=bc(Wt[:, d, s, 0:16].unsqueeze(2), [128, 16, 128]), op=ALU.mult), reads=[sk, Wtk], writes=[sk])
                    P.op("dve", lambda e, d=d, s3=s3: e.tensor_tensor(out=Hr[:, d, :, :], in0=Hr[:, d, :, :], in1=s3, op=ALU.add), reads=[sk, "Hr%d" % d], writes=["Hr%d" % d])
                else:
                    h0 = (pc - 1) * 32
                    s3 = st.rearrange("p (a b) -> p a b", a=32)
                    P.op("dve", lambda e, d=d, s=s, s3=s3, h0=h0: e.tensor_tensor(out=s3, in0=s3, in1=bc(Wt[:, d, s, 16 + h0:16 + h0 + 32].unsqueeze(2), [128, 32, 64]), op=ALU.mult), reads=[sk, Wtk], writes=[sk])
                    P.op("dve", lambda e, d=d, s3=s3, h0=h0: e.tensor_tensor(out=Hs[:, d, h0:h0 + 32, :], in0=Hs[:, d, h0:h0 + 32, :], in1=s3, op=ALU.add), reads=[sk, "Hs%d" % d], writes=["Hs%d" % d])
        P.op("act", lambda e, d=d: e.copy(out=Hrb[:, d, :, :], in_=Hr[:, d, :, :]), reads=["Hr%d" % d], writes=["Hrb%d" % d])
        P.op("act", lambda e, d=d: e.copy(out=Hsb[:, d, :, :], in_=Hs[:, d, :, :]), reads=["Hs%d" % d], writes=["Hsb%d" % d])
    P.barrier()
    AR.off = mix_end
    for t in range(NCH):
        scan_chunk(t, 0, lsrc, True, True)
    for t in range(NCH - 1, -1, -1):
        scan_chunk(t, 1, lsrc, True, False)
    P.barrier()

    AR.reset()
    rnwb, rnwk = AR.get([D], F32, "rnwb"); snwb, snwk = AR.get([4096], F32, "snwb")
    dma("sp", rnwb, rnw_d.partition_broadcast(128), [], [rnwk], "rnwb")
    dma("sp", snwb, snw_d.partition_broadcast(128), [], [snwk], "snwb")
    y_r = Ring(AR, 2, [6144], F32, "y6")
    g_r = Ring(AR, 2, [D], BF16, "g6"); z_r = Ring(AR, 2, [4096], BF16, "z6"); x6_r = Ring(AR, 2, [64, 64], BF16, "xs6")
    sg, sgk = AR.get([4096], F32, "sg")
    st8, st8k = AR.get([64], F32, "st8")
    yb_r = Ring(AR, 2, [6144], BF16, "yb")
    tT_r = Ring(AR, 2, [8, 128], BF16, "tT")
    for t in range(NCH):
        r0, r1 = t * 128, (t + 1) * 128
        y, yk = y_r.next(); gg, gk = g_r.next(); zz, zk = z_r.next(); x6, x6k = x6_r.next(); yb, ybk = yb_r.next()
        dma("sp", y, y_s[r0:r1, :], [], [yk], yk)
        dma("sp", gg, g_s[r0:r1, :], [], [gk], gk)
        dma("sp", zz, z_s[r0:r1, :], [], [zk], zk)
        dma("sp", x6.rearrange("p a b -> p (a b)"), xs_s[r0:r1, :], [], [x6k], x6k)
        yr3 = y[:, 0:2048].rearrange("p (a b) -> p a b", a=16)
        P.op("dve", lambda e, yr3=yr3: e.tensor_reduce(out=st8[:, 0:16], in_=yr3, axis=AX.X, op=ALU.add), reads=[yk], writes=[st8k])
        P.op("dve", lambda e: e.tensor_scalar(out=st8[:, 0:16], in0=st8[:, 0:16], scalar1=-1.0 / 128, scalar2=None, op0=ALU.mult), reads=[st8k], writes=[st8k])
        P.op("dve", lambda e, yr3=yr3: e.tensor_tensor(out=yr3, in0=yr3, in1=bc(st8[:, 0:16].unsqueeze(2), [128, 16, 128]), op=ALU.add), reads=[yk, st8k], writes=[yk])
        sg3 = sg[:, 0:2048].rearrange("p (a b) -> p a b", a=16)
        P.op("dve", lambda e, yr3=yr3, sg3=sg3: e.tensor_tensor(out=sg3, in0=yr3, in1=yr3, op=ALU.mult), reads=[yk], writes=[sgk])
        P.op("dve", lambda e, sg3=sg3: e.tensor_reduce(out=st8[:, 16:32], in_=sg3, axis=AX.X, op=ALU.add), reads=[sgk], writes=[st8k])
        P.op("dve", lambda e: e.tensor_scalar(out=st8[:, 16:32], in0=st8[:, 16:32], scalar1=1.0 / 128, scalar2=EPS, op0=ALU.mult, op1=ALU.add), reads=[st8k], writes=[st8k])
        P.op("act", lambda e: e.activation(out=st8[:, 16:32], in_=st8[:, 16:32], func=AF.Ln), reads=[st8k], writes=[st8k])
        P.op("act", lambda e: e.activation(out=st8[:, 16:32], in_=st8[:, 16:32], func=AF.Exp, scale=-0.5), reads=[st8k], writes=[st8k])
        P.op("dve", lambda e, yr3=yr3: e.tensor_tensor(out=yr3, in0=yr3, in1=bc(st8[:, 16:32].unsqueeze(2), [128, 16, 128]), op=ALU.mult), reads=[yk, st8k], writes=[yk])
        P.op("dve", lambda e, y=y: e.tensor_tensor(out=y[:, 0:2048], in0=y[:, 0:2048], in1=rnwb, op=ALU.mult), reads=[yk, rnwk], writes=[yk])
        P.op("act", lambda e, gg=gg: e.activation(out=sg[:, 0:2048], in_=gg, func=AF.Silu), reads=[gk, sgk], writes=[sgk])
        P.op("dve", lambda e, y=y, yb=yb: e.tensor_tensor(out=yb[:, 0:2048], in0=y[:, 0:2048], in1=sg[:, 0:2048], op=ALU.mult), reads=[yk, sgk], writes=[ybk])
        ys3 = y[:, 2048:6144].rearrange("p (a b) -> p a b", a=64)
        sg64 = sg.rearrange("p (a b) -> p a b", a=64)
        P.op("dve", lambda e, x6=x6, sg64=sg64: e.tensor_tensor(out=sg64, in0=x6, in1=bc(ddb[:].unsqueeze(2), [128, 64, 64]), op=ALU.mult), reads=[x6k, "ddb", sgk], writes=[sgk])
        P.op("dve", lambda e, y=y: e.tensor_tensor(out=y[:, 2048:6144], in0=y[:, 2048:6144], in1=sg, op=ALU.add), reads=[yk, sgk], writes=[yk])
        P.op("act", lambda e, zz=zz: e.activation(out=sg, in_=zz, func=AF.Silu), reads=[zk, sgk], writes=[sgk])
        P.op("dve", lambda e, y=y: e.tensor_tensor(out=y[:, 2048:6144], in0=y[:, 2048:6144], in1=sg, op=ALU.mult), reads=[yk, sgk], writes=[yk])
        P.op("dve", lambda e: e.memset(st8[:, 32:33], 0.0), reads=[st8k], writes=[st8k])
        P.op("act", lambda e, y=y: e.activation(out=sg, in_=y[:, 2048:6144], func=AF.Square, accum_out=st8[:, 32:33]), reads=[yk, sgk], writes=[sgk, st8k])
        P.op("dve", lambda e: e.tensor_scalar(out=st8[:, 32:33], in0=st8[:, 32:33], scalar1=1.0 / 4096, scalar2=EPS, op0=ALU.mult, op1=ALU.add), reads=[st8k], writes=[st8k])
        P.op("act", lambda e: e.activation(out=st8[:, 32:33], in_=st8[:, 32:33], func=AF.Ln), reads=[st8k], writes=[st8k])
        P.op("act", lambda e: e.activation(out=st8[:, 32:33], in_=st8[:, 32:33], func=AF.Exp, scale=-0.5), reads=[st8k], writes=[st8k])
        P.op("dve", lambda e, y=y, yb=yb: e.scalar_tensor_tensor(out=yb[:, 2048:6144], in0=y[:, 2048:6144], scalar=st8[:, 32:33], in1=snwb, op0=ALU.mult, op1=ALU.mult), reads=[yk, st8k, snwk], writes=[ybk])
        for g8 in range(6):
            pt, ptk = pt_next()
            tT, tTk = tT_r.next()

            def tr(e, g8=g8, pt=pt, yb=yb):
                r = None
                for q in range(8):
                    c = g8 * 8 + q
                    r = e.transpose(out=pt[:, q * 128:(q + 1) * 128], in_=yb[:, c * 128:(c + 1) * 128], identity=ident)
                return r
            P.op("pe", tr, reads=[ybk, "cstb"], writes=[ptk])
            P.op("act", lambda e, pt=pt, tT=tT: e.copy(out=tT.rearrange("p a b -> p (a b)"), in_=pt[:, :]), reads=[ptk], writes=[tTk])
            if g8 < 2:
                dma("sp", yrT_s[g8 * 8:(g8 + 1) * 8, :, r0:r1].rearrange("c p i -> p c i"), tT, [tTk], [], tTk)
            else:
                dma("sp", ysT_s[(g8 - 2) * 8:(g8 - 1) * 8, :, r0:r1].rearrange("c p i -> p c i"), tT, [tTk], [], tTk)
    P.barrier()

    def load_wg(Wd, k0, col0, wring, ncols=512):
        wt, wk = wring.next()
        for q4 in range(4):
            dma("pool", wt[:, q4 * 4:(q4 + 1) * 4, 0:ncols], Wd[k0 + q4 * 512:k0 + (q4 + 1) * 512, col0:col0 + ncols].rearrange("(kc p) n -> p kc n", p=128), [], [wk], wk)
        return wt, wk

    for tb in range(4):
        AR.reset()
        c0 = tb * 512
        gab, gabk = AR.get([D], F32, "gab"); gfb, gfbk = AR.get([D], F32, "gfb")
        h1, h1k = AR.get([4, D], F32, "h1")
        fT, fTk = AR.get([16, 512], BF16, "fT")
        if True:
            pass
        wr7 = Ring(AR, 2, [16, 512], BF16, "w7")
        gsrc, gsrck = AR.get([2, 16], BF16, "gsrc")
        ghl, ghlk = AR.get([2, 2, 16], BF16, "ghl")
        gtmp, gtmpk = AR.get([2, 16], F32, "gtmp")
        dg_r = Ring(AR, 2, [2, 128], BF16, "dg")
        tb_mark = AR.off
        yrT, yrTk = AR.get([16, 512], BF16, "yrT"); ysT, ysTk = AR.get([32, 512], BF16, "ysT")
        dma("sp", yrT, yrT_s[:, :, c0:c0 + 512].rearrange("c p i -> p c i"), [], [yrTk], yrTk)
        dma("sp", ysT, ysT_s[:, :, c0:c0 + 512].rearrange("c p i -> p c i"), [], [ysTk], ysTk)
        for wi, which in enumerate((2, 5)):
            P.op("dve", lambda e, wi=wi, which=which: e.tensor_copy(out=ghl[:, wi, 0, :], in_=modp[:, which, :, 0]), reads=["modp"], writes=[ghlk])
            P.op("dve", lambda e, wi=wi, which=which: e.tensor_tensor(out=gtmp[:, wi, :], in0=modp[:, which, :, 0], in1=ghl[:, wi, 0, :], op=ALU.subtract), reads=["modp", ghlk], writes=[gtmpk])
            P.op("dve", lambda e, wi=wi: e.tensor_copy(out=ghl[:, wi, 1, :], in_=gtmp[:, wi, :]), reads=[gtmpk], writes=[ghlk])
        for wi, (dstb, dstk) in enumerate(((gab, gabk), (gfb, gfbk))):
            for kq in range(4):
                ps, pk = ps_next()
                for k4 in range(4):
                    kc = kq * 4 + k4
                    dg, dgk = dg_r.next()
                    for hl in range(2):
                        P.op("dve", lambda e, dg=dg, hl=hl, wi=wi, kc=kc: e.tensor_scalar(out=dg[:, hl, :], in0=cstf[:, 0, :], scalar1=ghl[:, wi, hl, kc:kc + 1], scalar2=None, op0=ALU.mult), reads=["cstf", ghlk], writes=[dgk])

                    def fb(e, ps=ps, k4=k4, dg=dg):
                        e.matmul(ps[:, k4 * 128:(k4 + 1) * 128], lhsT=cstb[:, 5, :], rhs=dg[:, 0, :], start=True, stop=False)
                        return e.matmul(ps[:, k4 * 128:(k4 + 1) * 128], lhsT=cstb[:, 5, :], rhs=dg[:, 1, :], start=False, stop=True)
                    P.op("pe", fb, reads=[dgk, "cstb"], writes=[pk])
                P.op("act", lambda e, ps=ps, kq=kq, dstb=dstb: e.copy(out=dstb[:, kq * 512:(kq + 1) * 512], in_=ps[:, :]), reads=[pk], writes=[dstk])
        mT, mTk = AR.get([16, 512], BF16, "mT")
        bl_r_ = Ring(AR, 2, [2, 512], BF16, "bl")
        sgt_r = Ring(AR, 2, [2, 512], F32, "sgt")
        mtok_r = Ring(AR, 2, [512], BF16, "mtok")
        macc_r = Ring(AR, 2, [512], F32, "macc")
        for j in range(4):
            ps_list = [ps_next() for _ in range(4)]
            for half in range(2):
                wso, wsok = load_wg(w_so_d, half * 2048, j * 512, wr7)
                for tc4 in range(4):
                    ps, pk = ps_list[tc4]

                    def fso(e, ps=ps, tc4=tc4, half=half, wso=wso):
                        r = None
                        for kc in range(16):
                            r = e.matmul(ps[:, :], lhsT=ysT[:, half * 16 + kc, tc4 * 128:(tc4 + 1) * 128], rhs=wso[:, kc, :], start=(half == 0 and kc == 0), stop=(half == 1 and kc == 15))
                        return r
                    P.op("pe", fso, reads=[ysTk, wsok], writes=[pk])
            wro, wrok = load_wg(w_ro_d, 0, j * 512, wr7)
            for tc4 in range(4):
                ps2, pk2 = ps_list[tc4]
                ps, pk = ps_next()
                mm_a(ps, pk, yrT, yrTk, tc4 * 128, wro, wrok)
                r0 = c0 + tc4 * 128
                bl, blk_ = bl_r_.next(); sgt, sgtk = sgt_r.next(); macc, mak = macc_r.next()
                dma("sp", bl[:, 0, :], bl_s[r0:r0 + 128, j * 512:(j + 1) * 512], [], [blk_], blk_)
                dma("sp", bl[:, 1, :], bl_s[r0:r0 + 128, 2048 + j * 512:2048 + (j + 1) * 512], [], [blk_], blk_)
                P.op("act", lambda e, bl=bl, sgt=sgt: e.activation(out=sgt, in_=bl, func=AF.Sigmoid), reads=[blk_], writes=[sgtk])
                P.op("dve", lambda e, ps=ps, sgt=sgt, macc=macc: e.tensor_tensor(out=macc, in0=ps[:, :], in1=sgt[:, 0, :], op=ALU.mult), reads=[pk, sgtk], writes=[mak])
                mt_, mtk = mtok_r.next()
                P.op("dve", lambda e, ps2=ps2, sgt=sgt: e.tensor_tensor(out=sgt[:, 1, :], in0=ps2[:, :], in1=sgt[:, 1, :], op=ALU.mult), reads=[pk2, sgtk], writes=[sgtk])
                P.op("dve", lambda e, macc=macc, sgt=sgt, mt_=mt_: e.tensor_tensor(out=mt_, in0=macc, in1=sgt[:, 1, :], op=ALU.add), reads=[mak, sgtk], writes=[mtk])
                pt, ptk = pt_next()

                def tr(e, pt=pt, mt_=mt_):
                    r = None
                    for q in range(4):
                        r = e.transpose(out=pt[:, q * 128:(q + 1) * 128], in_=mt_[:, q * 128:(q + 1) * 128], identity=ident)
                    return r
                P.op("pe", tr, reads=[mtk, "cstb"], writes=[ptk])
                P.op("act", lambda e, pt=pt, j=j, tc4=tc4: e.copy(out=mT[:, j * 4:(j + 1) * 4, tc4 * 128:(tc4 + 1) * 128], in_=pt[:, 0:512].rearrange("p (a b) -> p a b", a=4)), reads=[ptk], writes=[mTk])
        if "mT_s" in DEBUG_OUT:
            dma("sp", mT_s[:, :, c0:c0 + 512].rearrange("c p i -> p c i"), mT, [mTk], [], "dbg_mT")
        xr_r = Ring(AR, 2, [512], F32, "xr")
        for tc4 in range(4):
            r0 = c0 + tc4 * 128
            dma("sp", h1[:, tc4, :], x_d[r0:r0 + 128, :], [], [h1k + "x%d" % tc4], h1k + "x%d" % tc4)
        for j in range(4):
            wo, wok = load_wg(w_o_d, 0, j * 512, wr7)
            for tc4 in range(4):
                ps, pk = ps_next()
                mm_a(ps, pk, mT, mTk, tc4 * 128, wo, wok)
                xr, xrk2 = xr_r.next()
                P.op("dve", lambda e, ps=ps, xr=xr, j=j: e.tensor_tensor(out=xr, in0=ps[:, :], in1=gab[:, j * 512:(j + 1) * 512], op=ALU.mult), reads=[pk, gabk], writes=[xrk2])
                P.op("dve", lambda e, xr=xr, j=j, tc4=tc4: e.tensor_tensor(out=h1[:, tc4, j * 512:(j + 1) * 512], in0=h1[:, tc4, j * 512:(j + 1) * 512], in1=xr, op=ALU.add), reads=[xrk2, h1k + "x%d" % tc4], writes=[h1k + "x%d" % tc4])
        fb_r = Ring(AR, 1, [D], BF16, "fb")
        for tc4 in range(4):
            hk = h1k + "x%d" % tc4
            fb_, fbk = fb_r.next()
            si = 40 + tc4
            P.op("dve", lambda e, si=si: e.memset(scl[:, si:si + 1], 0.0), writes=["sc%d" % si])
            P.op("act", lambda e, tc4=tc4, fb_=fb_, si=si: e.activation(out=fb_, in_=h1[:, tc4, :], func=AF.Square, accum_out=scl[:, si:si + 1]), reads=[hk], writes=[fbk, "sc%d" % si])
            P.op("dve", lambda e, si=si: e.tensor_scalar(out=scl[:, si:si + 1], in0=scl[:, si:si + 1], scalar1=1.0 / D, scalar2=EPS, op0=ALU.mult, op1=ALU.add), reads=["sc%d" % si], writes=["sc%d" % si])
            P.op("act", lambda e, si=si: e.activation(out=scl[:, si:si + 1], in_=scl[:, si:si + 1], func=AF.Ln), reads=["sc%d" % si], writes=["sc%d" % si])
            P.op("act", lambda e, si=si: e.activation(out=scl[:, si:si + 1], in_=scl[:, si:si + 1], func=AF.Exp, scale=-0.5), reads=["sc%d" % si], writes=["sc%d" % si])
            P.op("dve", lambda e, tc4=tc4, fb_=fb_, si=si: e.tensor_scalar(out=fb_, in0=h1[:, tc4, :], scalar1=scl[:, si:si + 1], scalar2=None, op0=ALU.mult), reads=[hk, "sc%d" % si, fbk], writes=[fbk])
            for g in range(2):
                pt, ptk = pt_next()

                def tr(e, g=g, pt=pt, fb_=fb_):
                    r = None
                    for q in range(8):
                        kc = g * 8 + q
                        r = e.transpose(out=pt[:, q * 128:(q + 1) * 128], in_=fb_[:, kc * 128:(kc + 1) * 128], identity=ident)
                    return r
                P.op("pe", tr, reads=[fbk, "cstb"], writes=[ptk])
                for q in range(8):
                    kc = g * 8 + q
                    P.op("act", lambda e, kc=kc, q=q, pt=pt, tc4=tc4: e.activation(out=fT[:, kc, tc4 * 128:(tc4 + 1) * 128], in_=pt[:, q * 128:(q + 1) * 128], func=AF.Identity, scale=w2e[:, kc:kc + 1], bias=modp[:, 3, kc, 0:1]), reads=[ptk, "w2e", "modp"], writes=[fTk])
        if "h1_s" in DEBUG_OUT:
            for tc4 in range(4):
                dma("sp", h1_s[c0 + tc4 * 128:c0 + (tc4 + 1) * 128, :], h1[:, tc4, :], [h1k + "x%d" % tc4], [], "dbg_h1")
            dma("sp", fT_s[:, :, c0:c0 + 512].rearrange("c p i -> p c i"), fT, [fTk], [], "dbg_fT")
        P.barrier()
        AR.off = tb_mark
        hidT, hidTk = AR.get([64, 512], BF16, "hidT")
        fnb, fnbk = AR.get([D], F32, "fnb")
        dma("sp", fnb, fnw_d.partition_broadcast(128), [], [fnbk], "fnb%d" % tb)
        xr_r = Ring(AR, 2, [512], F32, "xr2")
        rl_r = Ring(AR, 2, [512], F32, "rl")
        for j in range(16):
            w1t, w1k = load_wg(w1_d, 0, j * 512, wr7)
            for cb in range(4):
                ps, pk = ps_next()

                def f1(e, ps=ps, cb=cb, w1t=w1t):
                    r = None
                    for kc in range(16):
                        r = e.matmul(ps[:, :], lhsT=w1t[:, kc, cb * 128:(cb + 1) * 128], rhs=fT[:, kc, :], start=(kc == 0), stop=(kc == 15))
                    return r
                P.op("pe", f1, reads=[fTk, w1k], writes=[pk])
                rl, rlk = rl_r.next()
                P.op("act", lambda e, ps=ps, rl=rl: e.activation(out=rl, in_=ps[:, :], func=AF.Relu), reads=[pk], writes=[rlk])
                P.op("dve", lambda e, rl=rl, j=j, cb=cb: e.tensor_tensor(out=hidT[:, j * 4 + cb, :], in0=rl, in1=rl, op=ALU.mult), reads=[rlk], writes=[hidTk])
        ob_r = Ring(AR, 1, [D], F32, "ob")
        for j in range(4):
            ps_list = [ps_next() for _ in range(4)]
            for kg in range(4):
                w2t, w2k = load_wg(w2_d, kg * 2048, j * 512, wr7)
                for tc4 in range(4):
                    ps, pk = ps_list[tc4]

                    def f2(e, ps=ps, tc4=tc4, kg=kg, w2t=w2t):
                        r = None
                        for kc in range(16):
                            r = e.matmul(ps[:, :], lhsT=hidT[:, kg * 16 + kc, tc4 * 128:(tc4 + 1) * 128], rhs=w2t[:, kc, :], start=(kg == 0 and kc == 0), stop=(kg == 3 and kc == 15))
                        return r
                    P.op("pe", f2, reads=[hidTk, w2k], writes=[pk])
            for tc4 in range(4):
                ps, pk = ps_list[tc4]
                xr, xrk2 = xr_r.next()
                P.op("dve", lambda e, ps=ps, xr=xr, j=j: e.tensor_tensor(out=xr, in0=ps[:, :], in1=gfb[:, j * 512:(j + 1) * 512], op=ALU.mult), reads=[pk, gfbk], writes=[xrk2])
                P.op("dve", lambda e, xr=xr, j=j, tc4=tc4: e.tensor_tensor(out=h1[:, tc4, j * 512:(j + 1) * 512], in0=h1[:, tc4, j * 512:(j + 1) * 512], in1=xr, op=ALU.add), reads=[xrk2, h1k + "x%d" % tc4], writes=[h1k + "x%d" % tc4])
        for tc4 in range(4):
            hk = h1k + "x%d" % tc4
            ob, obk = ob_r.next()
            si = 44 + tc4
            r0 = c0 + tc4 * 128
            P.op("dve", lambda e, si=si: e.memset(scl[:, si:si + 1], 0.0), writes=["sc%d" % si])
            P.op("act", lambda e, tc4=tc4, ob=ob, si=si: e.activation(out=ob, in_=h1[:, tc4, :], func=AF.Square, accum_out=scl[:, si:si + 1]), reads=[hk], writes=[obk, "sc%d" % si])
            P.op("dve", lambda e, si=si: e.tensor_scalar(out=scl[:, si:si + 1], in0=scl[:, si:si + 1], scalar1=1.0 / D, scalar2=EPS, op0=ALU.mult, op1=ALU.add), reads=["sc%d" % si], writes=["sc%d" % si])
            P.op("act", lambda e, si=si: e.activation(out=scl[:, si:si + 1], in_=scl[:, si:si + 1], func=AF.Ln), reads=["sc%d" % si], writes=["sc%d" % si])
            P.op("act", lambda e, si=si: e.activation(out=scl[:, si:si + 1], in_=scl[:, si:si + 1], func=AF.Exp, scale=-0.5), reads=["sc%d" % si], writes=["sc%d" % si])
            P.op("dve", lambda e, tc4=tc4, ob=ob, si=si: e.scalar_tensor_tensor(out=ob, in0=h1[:, tc4, :], scalar=scl[:, si:si + 1], in1=fnb, op0=ALU.mult, op1=ALU.mult), reads=[hk, "sc%d" % si, fnbk, obk], writes=[obk])
            dma("sp", out_d[r0:r0 + 128, :], ob, [obk], ["outd"], obk)
        P.barrier()
    P.emit()
    return nc


def _consts():
    m = np.arange(128)[:, None].astype(np.float32)
    c = np.arange(128)[None, :].astype(np.float32)
    cst = np.zeros((128, 8, 128), np.float32)
    cst[:, 0] = (m == c)
    cst[:, 1] = (m <= c)
    cst[:, 2] = (m > c)
    cst[:, 3] = (m >= c)
    cst[:, 4] = (m < c)
    cst[:, 5] = 1.0
    cst[:, 6] = np.maximum(c - m, 0)
    cst[:, 7] = np.maximum(m - c, 0)
    return cst.reshape(128, 8 * 128)


def _rope_tables(core):
    pos = core * T + np.arange(T)
    row = (pos // 64).astype(np.float32)
    col = (pos % 64).astype(np.float32)
    freqs = (np.float32(10000.0) ** (-np.arange(32, dtype=np.float32) / np.float32(32))).astype(np.float32)
    ar = (row[:, None] * freqs).astype(np.float32)
    ac = (col[:, None] * freqs).astype(np.float32)
    ang = np.concatenate([ar, ac], axis=1)
    tab = np.stack([np.cos(ang), np.sin(ang)], 0).astype(np.float32)
    tab = tab.reshape(2, NCH, 128, 64).transpose(2, 0, 1, 3)
    return np.ascontiguousarray(tab)


def _core_consts(c):
    v = np.zeros(2 * 9 * 8 + 2 * 9 + 4, np.float32)
    for d in range(2):
        for s in range(9):
            for c2 in range(8):
                if d == 0:
                    on = (s < 8 and s < c2 < c) or (s == 8 and c2 < c)
                else:
                    on = (s < 8 and c < c2 < s) or (s == 8 and c2 > c)
                v[(d * 9 + s) * 8 + c2] = 1.0 if on else 0.0
            if d == 0:
                val = (s < c) if s < 8 else True
            else:
                val = (s > c) if s < 8 else True
            v[2 * 9 * 8 + d * 9 + s] = 1.0 if val else 0.0
    v[2 * 9 * 8 + 18] = 1.0 if c > 0 else 0.0
    v[2 * 9 * 8 + 19] = 1.0 if c < NCORES - 1 else 0.0
    return np.ascontiguousarray(np.broadcast_to(v[None, :], (128, v.size)))


_NC_CACHE = {}


def kernel(x, c, ctx, c_ctx, w_mod, b_mod, norm1_w, w_in, conv_w, conv_b, ret_decay_logit, ret_norm_w,
           ssd_a_log, ssd_dt_bias, ssd_d, ssd_norm_w, w_ret_out, w_ssd_out, w_o, norm2_w, w_mlp1, w_mlp2,
           final_norm_w):
    f = lambda a: np.ascontiguousarray(np.asarray(a, dtype=np.float32))
    x2 = f(x)[0]
    if "nc" not in _NC_CACHE:
        _NC_CACHE["nc"] = build_program()
    nc = _NC_CACHE["nc"]
    shared = {
        "ctx": f(ctx)[0], "cvec": np.ascontiguousarray(np.stack([f(c)[0], f(c_ctx)], 0)),
        "w_mod": f(w_mod)[0], "b_mod": np.ascontiguousarray(f(b_mod)[0].reshape(96, 128).T), "norm1_w": np.ascontiguousarray(f(norm1_w)[0].reshape(16, 128).T), "w_in": f(w_in)[0],
        "conv_w": np.ascontiguousarray(f(conv_w)[0].reshape(5, 48, 128).transpose(2, 1, 0)), "conv_b": np.ascontiguousarray(f(conv_b)[0].reshape(48, 128).T), "ret_decay_logit": f(ret_decay_logit)[0].reshape(32),
        "ret_norm_w": f(ret_norm_w)[0], "ssd_a_log": f(ssd_a_log)[0].reshape(128), "ssd_dt_bias": f(ssd_dt_bias)[0].reshape(128),
        "ssd_d": f(ssd_d)[0], "ssd_norm_w": f(ssd_norm_w)[0], "w_ret_out": f(w_ret_out)[0], "w_ssd_out": f(w_ssd_out)[0],
        "w_o": f(w_o)[0], "norm2_w": np.ascontiguousarray(f(norm2_w)[0].reshape(16, 128).T), "w_mlp1": f(w_mlp1)[0], "w_mlp2": f(w_mlp2)[0],
        "final_norm_w": f(final_norm_w), "cst": _consts(),
    }
    in_maps = []
    for ci in range(NCORES):
        m = dict(shared)
        m["x"] = np.ascontiguousarray(x2[ci * T:(ci + 1) * T])
        halo = np.zeros((4, D), np.float32)
        if ci > 0:
            halo[0:2] = x2[ci * T - 2:ci * T]
        if ci < NCORES - 1:
            halo[2:4] = x2[(ci + 1) * T:(ci + 1) * T + 2]
        m["x_halo"] = halo
        m["rope"] = _rope_tables(ci)
        m["corec"] = _core_consts(ci)
        in_maps.append(m)
    res = run_bass_kernel_spmd(nc, in_maps, core_ids=list(range(NCORES)))
    if DEBUG_OUT:
        _NC_CACHE["res"] = res.results
    out = np.concatenate([np.asarray(r["out"], dtype=np.float32) for r in res.results], axis=0)
    return out.reshape(1, NCORES * T, D)
```
